# Optimizing a Trainium2 kernel written in Bass

```python
import jax, jax.numpy as jnp
from jax import lax
import numpy as np

D_MODEL = 1024
BATCH = 8
SEQ = 4096
DEPTH = 2

CTX_LEN = 256
GRID_W = 64
HEAD_DIM = 64
NA_HEADS = 6
NA_KR = 8
NA_KW = 16
NA_QB = 16
NA_SPAN = NA_QB + NA_KW
SWA_HEADS = 6
SWA_KV_HEADS = 2
SWA_WINDOW = 128
SWA_BLOCK = 128
FNET_GROUPS = 4
FNET_GROUP_DIM = 64
NA_WIDTH = NA_HEADS * HEAD_DIM
SWA_WIDTH = SWA_HEADS * HEAD_DIM
SWA_KV_WIDTH = SWA_KV_HEADS * HEAD_DIM
FNET_WIDTH = FNET_GROUPS * FNET_GROUP_DIM
D_MIX = NA_WIDTH + SWA_WIDTH + FNET_WIDTH
D_IN = 3 * NA_WIDTH + SWA_WIDTH + 2 * SWA_KV_WIDTH + FNET_WIDTH
IN_SPLITS = (NA_WIDTH, 2 * NA_WIDTH, 3 * NA_WIDTH, 3 * NA_WIDTH + SWA_WIDTH,
             3 * NA_WIDTH + SWA_WIDTH + SWA_KV_WIDTH, 3 * NA_WIDTH + SWA_WIDTH + 2 * SWA_KV_WIDTH)
D_FF = 256 * ((8 * D_MODEL // 3 + 255) // 256)
N_SUB = 3
N_MOD = 3 * N_SUB
MACARON_WEIGHT = 0.5
ROPE_BASE = 10000.0
RMS_EPS = 1e-6
NEG_INF = -1e30

kernel_name = 'hybrid_natten_swa_fnet_macaron_dit'


def rms_norm(x, g):
    x32 = x.astype(jnp.float32)
    y = x32 * lax.rsqrt(jnp.mean(x32 * x32, axis=-1, keepdims=True) + RMS_EPS)
    return (y * g.astype(jnp.float32)).astype(x.dtype)


def modulate(h, shift, scale):
    return h * (1 + scale) + shift


def swiglu(h, w1, w2):
    gate, up = jnp.split(h @ w1, 2, axis=-1)
    return (jax.nn.silu(gate) * up) @ w2


def rope_half(x, pos):
    half = x.shape[-1] // 2
    inv = ROPE_BASE ** (-jnp.arange(half, dtype=jnp.float32) / half)
    ang = pos[:, None] * inv[None, :]
    cos = jnp.cos(ang)[:, None, :]
    sin = jnp.sin(ang)[:, None, :]
    x1, x2 = x[..., :half], x[..., half:]
    return jnp.concatenate([x1 * cos - x2 * sin, x1 * sin + x2 * cos], axis=-1)


def axial_rope(x):
    s = x.shape[1]
    t = jnp.arange(s)
    rows = (t // GRID_W).astype(jnp.float32)
    cols = (t % GRID_W).astype(jnp.float32)
    x32 = x.astype(jnp.float32)
    a = x.shape[-1] // 2
    out = jnp.concatenate([rope_half(x32[..., :a], rows), rope_half(x32[..., a:], cols)], axis=-1)
    return out.astype(x.dtype)


def _na_column_tables():
    n_jb = GRID_W // NA_QB
    k_start = np.clip(np.arange(n_jb) * NA_QB - NA_KW // 2, 0, GRID_W - NA_SPAN)
    col_idx = k_start[:, None] + np.arange(NA_SPAN)[None, :]
    q_col = np.arange(n_jb)[:, None] * NA_QB + np.arange(NA_QB)[None, :]
    w_start = np.clip(q_col - NA_KW // 2, 0, GRID_W - NA_KW)[..., None]
    k_col = col_idx[:, None, :]
    valid = (k_col >= w_start) & (k_col < w_start + NA_KW)
    offset = np.clip(k_col - q_col[..., None] + NA_KW - 1, 0, 2 * NA_KW - 2)
    return col_idx, valid, offset


def neighbourhood_attention(q, k, v, kc, vc, rpb):
    b, s, h, hd = q.shape
    rows = s // GRID_W
    kr = min(NA_KR, rows)
    n_jb = GRID_W // NA_QB
    n_lat = kr * NA_SPAN
    scale = hd ** -0.5
    col_idx, col_valid, col_off = _na_column_tables()
    qg = q.reshape(b, rows, n_jb, NA_QB, h, hd)
    kg = k.reshape(b, rows, GRID_W, h, hd)[:, :, col_idx]
    vg = v.reshape(b, rows, GRID_W, h, hd)[:, :, col_idx]
    rpb_col = rpb[:, :, col_off]
    valid = jnp.asarray(col_valid)[None, None, :, :, None, :]

    def row_step(r):
        rs = jnp.clip(r - kr // 2, 0, rows - kr)
        q_r = lax.dynamic_index_in_dim(qg, r, axis=1, keepdims=False)
        k_r = lax.dynamic_slice_in_dim(kg, rs, kr, axis=1)
        v_r = lax.dynamic_slice_in_dim(vg, rs, kr, axis=1)
        row_off = rs + jnp.arange(kr) - r + NA_KR - 1
        bias = jnp.transpose(rpb_col[:, row_off], (0, 2, 3, 1, 4)).astype(jnp.float32)
        s_lat = jnp.einsum('bjqhd,bijshd->bhjqis', q_r, k_r).astype(jnp.float32) * scale + bias
        s_lat = jnp.where(valid, s_lat, NEG_INF).reshape(b, h, n_jb, NA_QB, n_lat)
        s_ctx = jnp.einsum('bjqhd,blhd->bhjql', q_r, kc).astype(jnp.float32) * scale
        p = jax.nn.softmax(jnp.concatenate([s_lat, s_ctx], axis=-1), axis=-1).astype(v.dtype)
        p_lat = p[..., :n_lat].reshape(b, h, n_jb, NA_QB, kr, NA_SPAN)
        p_ctx = p[..., n_lat:]
        return (jnp.einsum('bhjqis,bijshd->bjqhd', p_lat, v_r)
                + jnp.einsum('bhjql,blhd->bjqhd', p_ctx, vc))

    out = lax.map(row_step, jnp.arange(rows))
    return jnp.moveaxis(out, 0, 1).reshape(b, s, h * hd)


def window_gqa_attention(q, k, v, kc, vc, sink):
    b, s, h, hd = q.shape
    kvh = k.shape[2]
    g = h // kvh
    n_ctx = kc.shape[1]
    nb = s // SWA_BLOCK
    span = 3 * SWA_BLOCK
    scale = hd ** -0.5
    qb = q.reshape(b, nb, SWA_BLOCK, kvh, g, hd)
    pad = ((0, 0), (SWA_BLOCK, SWA_BLOCK), (0, 0), (0, 0))
    kp = jnp.pad(k, pad)
    vp = jnp.pad(v, pad)
    rel = jnp.arange(span)[None, :] - SWA_BLOCK - jnp.arange(SWA_BLOCK)[:, None]
    in_window = jnp.abs(rel) <= SWA_WINDOW
    sink_l = jnp.broadcast_to(sink.astype(jnp.float32).reshape(1, kvh, g, 1, 1), (b, kvh, g, SWA_BLOCK, 1))

    def block_step(n):
        q_n = lax.dynamic_index_in_dim(qb, n, axis=1, keepdims=False)
        k_n = lax.dynamic_slice_in_dim(kp, n * SWA_BLOCK, span, axis=1)
        v_n = lax.dynamic_slice_in_dim(vp, n * SWA_BLOCK, span, axis=1)
        kpos = (n - 1) * SWA_BLOCK + jnp.arange(span)
        valid = in_window & ((kpos >= 0) & (kpos < s))[None, :]
        s_lat = jnp.einsum('bqkgd,bskd->bkgqs', q_n, k_n).astype(jnp.float32) * scale
        s_lat = jnp.where(valid, s_lat, NEG_INF)
        s_ctx = jnp.einsum('bqkgd,blkd->bkgql', q_n, kc).astype(jnp.float32) * scale
        p = jax.nn.softmax(jnp.concatenate([s_lat, s_ctx, sink_l], axis=-1), axis=-1).astype(v.dtype)
        return (jnp.einsum('bkgqs,bskd->bqkgd', p[..., :span], v_n)
                + jnp.einsum('bkgql,blkd->bqkgd', p[..., span:span + n_ctx], vc))

    out = lax.map(block_step, jnp.arange(nb))
    return jnp.moveaxis(out, 0, 1).reshape(b, s, h * hd)


def context_attention(qc, kc, vc, sink=None):
    b, l, h, hd = qc.shape
    kvh = kc.shape[2]
    g = h // kvh
    q5 = qc.reshape(b, l, kvh, g, hd)
    sc = jnp.einsum('blkgd,bmkd->bkglm', q5, kc).astype(jnp.float32) * hd ** -0.5
    if sink is not None:
        sk = jnp.broadcast_to(sink.astype(jnp.float32).reshape(1, kvh, g, 1, 1), (b, kvh, g, l, 1))
        sc = jnp.concatenate([sc, sk], axis=-1)
    p = jax.nn.softmax(sc, axis=-1)[..., :l].astype(vc.dtype)
    return jnp.einsum('bkglm,bmkd->blkgd', p, vc).reshape(b, l, h * hd)


def fourier_mix(u):
    b, n, _ = u.shape
    u4 = u.astype(jnp.float32).reshape(b, n, FNET_GROUPS, FNET_GROUP_DIM)
    f = jnp.fft.fft2(u4, axes=(1, 3), norm='ortho').real
    return f.reshape(b, n, FNET_WIDTH).astype(u.dtype)


def project_groups(h, w_in):
    b, n, _ = h.shape
    aq, ak, av, bq, bk, bv, fu = jnp.split(h @ w_in, IN_SPLITS, axis=-1)
    heads = lambda t, nh: t.reshape(b, n, nh, HEAD_DIM)
    return (heads(aq, NA_HEADS), heads(ak, NA_HEADS), heads(av, NA_HEADS),
            heads(bq, SWA_HEADS), heads(bk, SWA_KV_HEADS), heads(bv, SWA_KV_HEADS), fu)


def mods_of(m, sub):
    return m[..., 3 * sub, :], m[..., 3 * sub + 1, :], m[..., 3 * sub + 2, :]


def ffn_sublayer(h, m, sub, g_pre, g_post, w1, w2):
    shift, scale, gate = mods_of(m, sub)
    y = swiglu(modulate(rms_norm(h, g_pre[sub]), shift, scale), w1, w2)
    return h + MACARON_WEIGHT * gate * rms_norm(y, g_post[sub])


def hybrid_layer(x, xc, mod, mod_c, g_pre, g_post, w_ffn_in, w_ffn_out, w_in, w_out, na_rpb, swa_sink, ctx_out):
    x = ffn_sublayer(x, mod, 0, g_pre, g_post, w_ffn_in[0], w_ffn_out[0])
    xc = ffn_sublayer(xc, mod_c, 0, g_pre, g_post, w_ffn_in[0], w_ffn_out[0])
    sh, sc, gt = mods_of(mod, 1)
    shc, scc, gtc = mods_of(mod_c, 1)
    h = modulate(rms_norm(x, g_pre[1]), sh, sc)
    hc = modulate(rms_norm(xc, g_pre[1]), shc, scc)
    aq, ak, av, bq, bk, bv, fu = project_groups(h, w_in)
    aqc, akc, avc, bqc, bkc, bvc, fuc = project_groups(hc, w_in)
    bq = axial_rope(bq)
    bk = axial_rope(bk)
    o_a = neighbourhood_attention(aq, ak, av, akc, avc, na_rpb)
    o_b = window_gqa_attention(bq, bk, bv, bkc, bvc, swa_sink)
    o_c = fourier_mix(fu)
    o = jnp.concatenate([o_a, o_b, o_c], axis=-1) @ w_out
    x = x + gt * rms_norm(o, g_post[1])
    x = ffn_sublayer(x, mod, 2, g_pre, g_post, w_ffn_in[1], w_ffn_out[1])
    if ctx_out:
        oc = jnp.concatenate([context_attention(aqc, akc, avc),
                              context_attention(bqc, bkc, bvc, swa_sink),
                              fourier_mix(fuc)], axis=-1) @ w_out
        xc = xc + gtc * rms_norm(oc, g_post[1])
        xc = ffn_sublayer(xc, mod_c, 2, g_pre, g_post, w_ffn_in[1], w_ffn_out[1])
    else:
        xc = None
    return x, xc


def setup_inputs(seed: int = 0) -> dict:
    key = jax.random.key(seed)
    ks = jax.random.split(key, 14)
    nrm = jax.random.normal
    f32 = jnp.float32
    return {
        'x': nrm(ks[0], (BATCH, SEQ, D_MODEL), f32),
        'c': nrm(ks[1], (BATCH, D_MODEL), f32),
        'ctx': nrm(ks[2], (BATCH, CTX_LEN, D_MODEL), f32),
        'c_ctx': nrm(ks[3], (D_MODEL,), f32),
        'w_mod': nrm(ks[4], (DEPTH, D_MODEL, N_MOD * D_MODEL), f32) * (0.5 * D_MODEL ** -0.5),
        'b_mod': nrm(ks[5], (DEPTH, N_MOD * D_MODEL), f32) * 0.01,
        'g_pre': 1.0 + 0.01 * nrm(ks[6], (DEPTH, N_SUB, D_MODEL), f32),
        'g_post': 1.0 + 0.01 * nrm(ks[7], (DEPTH, N_SUB, D_MODEL), f32),
        'w_ffn_in': nrm(ks[8], (DEPTH, 2, D_MODEL, 2 * D_FF), f32) * D_MODEL ** -0.5,
        'w_ffn_out': nrm(ks[9], (DEPTH, 2, D_FF, D_MODEL), f32) * D_FF ** -0.5,
        'w_in': nrm(ks[10], (DEPTH, D_MODEL, D_IN), f32) * D_MODEL ** -0.5,
        'w_out': nrm(ks[11], (DEPTH, D_MIX, D_MODEL), f32) * D_MIX ** -0.5,
        'na_rpb': nrm(ks[12], (DEPTH, NA_HEADS, 2 * NA_KR - 1, 2 * NA_KW - 1), f32) * 0.1,
        'swa_sink': nrm(ks[13], (DEPTH, SWA_HEADS), f32) * 0.5,
    }


def reference(x, c, ctx, c_ctx, w_mod, b_mod, g_pre, g_post, w_ffn_in, w_ffn_out, w_in, w_out, na_rpb, swa_sink):
    b = x.shape[0]
    xc = ctx
    c_act = jax.nn.silu(c)
    cc_act = jax.nn.silu(c_ctx)
    for layer in range(DEPTH):
        mod = (c_act @ w_mod[layer] + b_mod[layer]).reshape(b, 1, N_MOD, D_MODEL)
        mod_c = (cc_act @ w_mod[layer] + b_mod[layer]).reshape(N_MOD, D_MODEL)
        x, xc = hybrid_layer(x, xc, mod, mod_c, g_pre[layer], g_post[layer], w_ffn_in[layer], w_ffn_out[layer],
                             w_in[layer], w_out[layer], na_rpb[layer], swa_sink[layer],
                             ctx_out=(layer < DEPTH - 1))
    return x
```

```python
from contextlib import ExitStack
import numpy as np
import ml_dtypes
import concourse.bass as bass
import concourse.mybir as mybir
from concourse.bass_utils import run_bass_kernel_spmd

F32 = mybir.dt.float32
BF16 = mybir.dt.bfloat16
ALU = mybir.AluOpType
AF = mybir.ActivationFunctionType

D = 1024
S = 4096
CTX = 256
NTOK = S + CTX
DEPTH = 2
DFF = 2816
NJ = DFF // 128
EPS = 1e-6
NEG = -30000.0
SAME_ENG_SYNC = True


class Buf:
    __slots__ = ("name", "w", "r")

    def __init__(self, name=""):
        self.name = name
        self.w = None
        self.r = {}


class Sem:
    __slots__ = ("h", "count")

    def __init__(self, h):
        self.h = h
        self.count = 0


class SemPool:
    def __init__(self, nc, ndma=16):
        self.eng = {e: Sem(nc.alloc_semaphore(name="g_" + e)) for e in ("pe", "act", "dve", "pool")}
        self.dma = [Sem(nc.alloc_semaphore(name=f"g_dma{i}")) for i in range(ndma)]
        self.swdma = [Sem(nc.alloc_semaphore(name=f"g_swdma{i}")) for i in range(8)]


_POOL = [None]


class Phase:
    ENGS = ("pe", "act", "dve", "pool", "sp")

    def __init__(self, nc, name):
        self.nc = nc
        self.name = name
        self.es = ExitStack()
        self.q = {e: [] for e in self.ENGS}
        self.waited = {e: {} for e in self.ENGS}
        self.esem = {}
        self.dsems = []
        self.bufs = []
        self.nsem = 0
        self.allsems = []
        self.ndma = 0
        self.nsw = 0
        for e in ("pe", "act", "dve", "pool"):
            self.esem[e] = _POOL[0].eng[e]

    def sem(self, name):
        self.nsem += 1
        h = self.nc.alloc_semaphore(name=f"{self.name}_{name}_{self.nsem}")
        self.allsems.append(h)
        return Sem(h)

    def dsem(self, name="d"):
        s = _POOL[0].dma[self.ndma]
        self.ndma += 1
        self.dsems.append(s)
        return s

    def dsem_sw(self, name="d"):
        s = _POOL[0].swdma[self.nsw]
        self.nsw += 1
        self.dsems.append(s)
        return s

    def buf(self, name=""):
        b = Buf(name)
        self.bufs.append(b)
        return b

    def track(self, *bs):
        for b in bs:
            b.w = None
            b.r = {}
            self.bufs.append(b)

    def sb(self, name, shape, dt):
        return self.es.enter_context(self.nc.sbuf_tensor(f"{self.name}_{name}", list(shape), dt))

    def ps(self, name, shape=(128, 512), dt=F32):
        return self.es.enter_context(self.nc.psum_tensor(f"{self.name}_ps_{name}", list(shape), dt))

    def _waits(self, eng, reads, writes, skip=None):
        need = {}

        def add(tok):
            if tok is None:
                return
            s, v = tok
            if s is self.esem.get(eng) and (eng == "pe" or not SAME_ENG_SYNC):
                return
            if s is skip:
                return
            if need.get(s, 0) < v:
                need[s] = v

        for b in reads:
            add(b.w)
        for b in writes:
            add(b.w)
            for s, v in b.r.items():
                add((s, v))
        wl = []
        wd = self.waited[eng]
        for s, v in need.items():
            if wd.get(s, 0) < v:
                wd[s] = v
                wl.append((s.h, v))
        return wl

    @staticmethod
    def _mark(tok, reads, writes):
        s, v = tok
        for b in reads:
            if b.r.get(s, 0) < v:
                b.r[s] = v
        for b in writes:
            b.w = tok
            b.r = {}

    def op(self, eng, fn, reads=(), writes=()):
        wl = self._waits(eng, reads, writes)
        es = self.esem[eng]
        es.count += 1
        self._mark((es, es.count), reads, writes)
        h = es.h

        def thunk(e):
            for sh, v in wl:
                e.wait_ge(sh, v)
            fn(e).then_inc(h, 1)

        self.q[eng].append(thunk)

    def mm(self, out_ap, lhsT, rhs, start, stop, reads=(), writes=()):
        wl = self._waits("pe", reads, writes if start else ())
        es = self.esem["pe"]
        tok = (es, es.count + 1)
        self._mark(tok, reads, writes if stop else ())
        if not stop:
            for b in writes:
                pass
        h = es.h
        if stop:
            es.count += 1

        def thunk(e):
            for sh, v in wl:
                e.wait_ge(sh, v)
            ins = e.matmul(out_ap, lhsT, rhs, start=start, stop=stop)
            if stop:
                ins.then_inc(h, 1)

        self.q["pe"].append(thunk)

    def dma(self, q, out_ap, in_ap, sem, reads=(), writes=()):
        wl = self._waits(q, reads, writes, skip=sem)
        sem.count += 16
        self._mark((sem, sem.count), reads, writes)
        h = sem.h

        def thunk(e):
            for sh, v in wl:
                e.wait_ge(sh, v)
            e.dma_start(out=out_ap, in_=in_ap).then_inc(h, 16)

        self.q[q].append(thunk)

    def finish(self):
        fin = [(s.h, s.count) for s in self.dsems if s.count > 0]

        def thunk(e):
            for sh, v in fin:
                e.wait_ge(sh, v)

        self.q["sp"].append(thunk)
        q = self.q
        with self.nc.Block() as blk:
            @blk.tensor
            def _(e):
                for t in q["pe"]:
                    t(e)

            @blk.scalar
            def _(e):
                for t in q["act"]:
                    t(e)

            @blk.vector
            def _(e):
                for t in q["dve"]:
                    t(e)

            @blk.gpsimd
            def _(e):
                for t in q["pool"]:
                    t(e)

            @blk.sync
            def _(e):
                for t in q["sp"]:
                    t(e)
        for b in self.bufs:
            b.w = None
            b.r = {}
        self.es.close()


class Cfg:
    def __init__(self, depth=DEPTH, ffn_tiles=None, stop_after=None, ncores=8, stage=9, proj_tiles=None,
                 att_tiles=None, att_pairs=None, fnet_tiles=None, phases=None, debug=False):
        self.proj_tiles, self.att_tiles, self.att_pairs, self.fnet_tiles = proj_tiles, att_tiles, att_pairs, fnet_tiles
        self.phases = phases
        self.debug = debug
        self.stage = stage
        self.ncores = ncores
        self.depth = depth
        self.ffn_tiles = ffn_tiles
        self.stop_after = stop_after


def token_tiles(n):
    tiles = [(t0, n, 0) for t0 in range(0, S, n)]
    tiles += [(S + t0, min(n, CTX - t0), 1) for t0 in range(0, CTX, n)]
    return tiles


def mod_phase(nc, G, dr):
    for l in range(DEPTH):
        P = Phase(nc, f"mod{l}")
        P.track(G["b"])
        cin = P.sb("cin", [128, 8, 2], F32)
        cact = P.sb("cact", [128, 8, 2], BF16)
        bm = P.sb("bm", [128, 72], F32)
        gp = P.sb("gp", [128, 2, 3, 8], F32)
        modT = P.sb("modT", [128, 72, 2], F32)
        wbuf = [P.sb(f"w{i}", [128, 8, 2304], BF16) for i in range(2)]
        psm = P.ps("psm", [128, 512])
        b_cin, b_cact, b_bm, b_gp, b_mod, b_ps = (P.buf() for _ in range(6))
        b_w = [P.buf(), P.buf()]
        s_small = P.dsem("small")
        s_w = [P.dsem_sw("w0"), P.dsem_sw("w1")]
        P.dma("sp", cin[:], dr["cin"][:], s_small, writes=[b_cin])
        P.dma("sp", bm[:], dr["bmod"][l], P.dsem("bm"), writes=[b_bm])
        P.dma("sp", gp[:], dr["gpp"][l], P.dsem("gp"), writes=[b_gp])
        P.op("act", lambda e: e.activation(cact[:], cin[:], AF.Silu), reads=[b_cin], writes=[b_cact])
        wsrc = dr["w_mod"][l].rearrange("(kc p) n -> p kc n", p=128)
        for g in range(4):
            i = g % 2
            P.dma("pool", wbuf[i][:], wsrc[:, :, g * 2304:(g + 1) * 2304], s_w[i], writes=[b_w[i]])
            for nn in range(18):
                ncol = g * 18 + nn
                for kc in range(8):
                    P.mm(psm[:, 2 * ncol:2 * ncol + 2], wbuf[i][:, kc, nn * 128:(nn + 1) * 128], cact[:, kc, :],
                         start=(kc == 0), stop=(kc == 7), reads=[b_w[i], b_cact], writes=[b_ps])
        ps3 = psm[:, 0:144].rearrange("p (n j) -> p n j", j=2)
        for j in range(2):
            P.op("dve", lambda e, j=j: e.tensor_tensor(modT[:, :, j], ps3[:, :, j], bm[:], ALU.add),
                 reads=[b_ps, b_bm], writes=[b_mod])
        A, B, Gt = G["A"], G["B"], G["Gt"]
        for sub in range(3):
            for j in range(2):
                sh = modT[:, (3 * sub) * 8:(3 * sub) * 8 + 8, j]
                sc = modT[:, (3 * sub + 1) * 8:(3 * sub + 1) * 8 + 8, j]
                gt = modT[:, (3 * sub + 2) * 8:(3 * sub + 2) * 8 + 8, j]
                wgt = 32.0 * (0.5 if sub != 1 else 1.0)
                P.op("dve", lambda e, sc=sc, sub=sub, j=j: e.scalar_tensor_tensor(
                    A[:, l, sub, j, :], sc, 1.0, gp[:, 0, sub, :], ALU.add, ALU.mult), reads=[b_mod, b_gp], writes=[G["b"]])
                P.op("dve", lambda e, sub=sub, j=j: e.tensor_scalar(
                    A[:, l, sub, j, :], A[:, l, sub, j, :], 32.0, None, ALU.mult), reads=[G["b"]], writes=[G["b"]])
                P.op("dve", lambda e, sh=sh, sub=sub, j=j: e.tensor_copy(B[:, l, sub, j, :], sh),
                     reads=[b_mod], writes=[G["b"]])
                P.op("dve", lambda e, gt=gt, sub=sub, j=j, wgt=wgt: e.scalar_tensor_tensor(
                    Gt[:, l, sub, j, :], gt, wgt, gp[:, 1, sub, :], ALU.mult, ALU.mult), reads=[b_mod, b_gp], writes=[G["b"]])
        P.finish()


def rms_rstd(P, ps_ss, rstd, n, b_ps, b_rstd, epsb):
    P.op("act", lambda e: e.activation(rstd[:, :n], ps_ss[:, :n], AF.Sqrt, bias=epsb[:, 0:1], scale=1.0),
         reads=[b_ps], writes=[b_rstd])
    P.op("dve", lambda e: e.reciprocal(rstd[:, :n], rstd[:, :n]), reads=[b_rstd], writes=[b_rstd])


def ffn_phase(nc, G, dr, l, f, cfg, skip_ctx=False):
    sub = 0 if f == 0 else 2
    P = Phase(nc, f"ffn{l}{f}")
    N = 512
    XT = dr["XT"]
    w1 = P.sb("w1", [128, 8, 2 * DFF], BF16)
    w2 = P.sb("w2", [128, NJ, D], BF16)
    xt = P.sb("xt", [128, 8, N], F32)
    sg = P.sb("sg", [128, N], F32)
    h = [P.sb(f"h{i}", [128, 8, N], BF16) for i in range(2)]
    u = P.sb("u", [128, NJ, N], BF16)
    y = P.sb("y", [128, 8, N], F32)
    rstd = P.sb("rstd", [128, N], F32)
    ps_ss = P.ps("ss")
    ps_g = [P.ps(f"g{i}") for i in range(2)]
    ps_u = [P.ps(f"u{i}") for i in range(2)]
    ps_y = [P.ps(f"y{i}") for i in range(2)]
    JG = [(0, 2), (2, 8), (8, 15), (15, 22)]
    b_w1g = [P.buf() for _ in range(4)]
    b_w2 = P.buf()
    b_sg, b_u, b_rstd, b_ss = (P.buf() for _ in range(4))
    b_xt = [P.buf() for _ in range(8)]
    b_y = [P.buf() for _ in range(8)]
    b_h = [[P.buf() for _ in range(8)] for _ in range(2)]
    b_pg = [P.buf(), P.buf()]
    b_pu = [P.buf(), P.buf()]
    b_py = [P.buf(), P.buf()]
    s_w1 = [P.dsem_sw(f"w1{i}") for i in range(4)]
    s_w2 = P.dsem_sw("w2")
    s_ld = P.dsem("ld")
    s_st = P.dsem("st")
    ones = G["ones"]
    A, B, Gt = G["A"], G["B"], G["Gt"]
    P.track(G["b"])
    jgrp = {}
    for gi, (j0, j1) in enumerate(JG):
        for jj in range(j0, j1):
            jgrp[jj] = gi

    w1src = dr["w_ffn_in"][l, f].rearrange("(kc p) n -> p kc n", p=128)
    w2src = dr["w_ffn_out"][l, f].rearrange("(j p) n -> p j n", p=128)
    for gi, (j0, j1) in enumerate(JG):
        for off in (0, DFF):
            P.dma("pool", w1[:, :, off + j0 * 128:off + j1 * 128], w1src[:, :, off + j0 * 128:off + j1 * 128], s_w1[gi],
                  writes=[b_w1g[gi]])
    for hh in range(2):
        P.dma("pool", w2[:, hh * 11:(hh + 1) * 11, :], w2src[:, hh * 11:(hh + 1) * 11, :], s_w2, writes=[b_w2])

    tiles = token_tiles(N)
    if skip_ctx:
        tiles = [t for t in tiles if t[2] == 0]
    if cfg.ffn_tiles is not None:
        tiles = tiles[:cfg.ffn_tiles]
    nt = len(tiles)

    XS = dr["XTin"] if (l == 0 and f == 0 and cfg.phases is None) else XT

    def xsrc(ti):
        t0, n, j = tiles[ti]
        return XS[:, t0:t0 + n].rearrange("(kc p) t -> p kc t", p=128)

    def xdst(ti):
        t0, n, j = tiles[ti]
        return XT[:, t0:t0 + n].rearrange("(kc p) t -> p kc t", p=128)

    def H_load(ti):
        t0, n, j = tiles[ti]
        P.dma("sp", xt[:, :, :n], xsrc(ti), s_ld, writes=b_xt)

    def H_sq(ti):
        t0, n, j = tiles[ti]
        hb = ti % 2
        P.op("act", lambda e: e.activation(h[hb][:, :, :n], xt[:, :, :n], AF.Square), reads=b_xt, writes=b_h[hb])

    def H_pre(ti):
        t0, n, j = tiles[ti]
        hb = ti % 2
        for kc in range(8):
            P.mm(ps_ss[:, :n], ones[:], h[hb][:, kc, :n], start=(kc == 0), stop=(kc == 7), reads=[b_h[hb][kc]], writes=[b_ss])
        rms_rstd(P, ps_ss, rstd, n, b_ss, b_rstd, G["epsb"])

    def H_c(ti, kc):
        t0, n, j = tiles[ti]
        hb = ti % 2
        P.op("dve", lambda e: e.tensor_tensor(xt[:, kc, :n], xt[:, kc, :n], rstd[:, :n], ALU.mult),
             reads=[b_xt[kc], b_rstd], writes=[b_xt[kc]])
        P.op("act", lambda e: e.activation(
            h[hb][:, kc, :n], xt[:, kc, :n], AF.Identity, bias=B[:, l, sub, j, kc:kc + 1], scale=A[:, l, sub, j, kc:kc + 1]),
            reads=[b_xt[kc], G["b"]], writes=[b_h[hb][kc]])

    def M_gu(ti, jj):
        t0, n, j = tiles[ti]
        hb = ti % 2
        pb = jj % 2
        bw = b_w1g[jgrp[jj]]
        for kc in range(8):
            P.mm(ps_g[pb][:, :n], w1[:, kc, jj * 128:(jj + 1) * 128], h[hb][:, kc, :n], start=(kc == 0), stop=(kc == 7),
                 reads=[bw, b_h[hb][kc]], writes=[b_pg[pb]])
        for kc in range(8):
            P.mm(ps_u[pb][:, :n], w1[:, kc, DFF + jj * 128:DFF + (jj + 1) * 128], h[hb][:, kc, :n], start=(kc == 0),
                 stop=(kc == 7), reads=[bw, b_h[hb][kc]], writes=[b_pu[pb]])
        P.op("act", lambda e: e.activation(sg[:, :n], ps_g[pb][:, :n], AF.Silu), reads=[b_pg[pb]], writes=[b_sg])
        P.op("dve", lambda e: e.tensor_tensor(u[:, jj, :n], sg[:, :n], ps_u[pb][:, :n], ALU.mult),
             reads=[b_sg, b_pu[pb]], writes=[b_u])

    def M_y(ti, c):
        t0, n, j = tiles[ti]
        hb = ti % 2
        pb = c % 2
        for jj in range(NJ):
            P.mm(ps_y[pb][:, :n], w2[:, jj, c * 128:(c + 1) * 128], u[:, jj, :n], start=(jj == 0), stop=(jj == NJ - 1),
                 reads=[b_w2, b_u], writes=[b_py[pb]])
        P.op("act", lambda e: e.activation(h[hb][:, c, :n], ps_y[pb][:, :n], AF.Square), reads=[b_py[pb]], writes=[b_h[hb][c]])
        P.op("act", lambda e: e.activation(y[:, c, :n], ps_y[pb][:, :n], AF.Identity, scale=Gt[:, l, sub, j, c:c + 1]),
             reads=[b_py[pb], G["b"]], writes=[b_y[c]])

    def T_reload(ti):
        t0, n, j = tiles[ti]
        P.dma("sp", xt[:, :, :n], xsrc(ti), s_ld, writes=b_xt)

    def T_pre(ti):
        t0, n, j = tiles[ti]
        hb = ti % 2
        for c in range(8):
            P.mm(ps_ss[:, :n], ones[:], h[hb][:, c, :n], start=(c == 0), stop=(c == 7), reads=[b_h[hb][c]], writes=[b_ss])
        rms_rstd(P, ps_ss, rstd, n, b_ss, b_rstd, G["epsb"])

    def T_c(ti, c):
        t0, n, j = tiles[ti]
        P.op("dve", lambda e: e.tensor_tensor(y[:, c, :n], y[:, c, :n], rstd[:, :n], ALU.mult),
             reads=[b_y[c], b_rstd], writes=[b_y[c]])
        P.op("pool", lambda e: e.tensor_tensor(xt[:, c, :n], xt[:, c, :n], y[:, c, :n], ALU.add),
             reads=[b_y[c], b_xt[c]], writes=[b_xt[c]])

    def T_store(ti):
        t0, n, j = tiles[ti]
        P.dma("sp", xdst(ti), xt[:, :, :n], s_st, reads=b_xt)

    H_load(0)
    H_sq(0)
    H_pre(0)
    for kc in range(8):
        H_c(0, kc)
    if nt == 1:
        T_reload(0)
    for ti in range(nt):
        nxt = ti + 1 < nt
        for jj in range(NJ):
            M_gu(ti, jj)
            if ti >= 1:
                if jj == 1:
                    T_pre(ti - 1)
                if 2 <= jj <= 9:
                    T_c(ti - 1, jj - 2)
                if jj == 10:
                    T_store(ti - 1)
                if jj == 11 and not nxt:
                    T_reload(ti)
            if nxt:
                if jj == 10:
                    H_load(ti + 1)
                if jj == 16:
                    H_sq(ti + 1)
                if jj == 17:
                    H_pre(ti + 1)
                if jj >= 18:
                    H_c(ti + 1, jj - 18)
        for c in range(8):
            M_y(ti, c)
            if nxt and c < 4:
                H_c(ti + 1, 4 + c)
            if nxt and c == 3:
                T_reload(ti)
    T_pre(nt - 1)
    for c in range(8):
        T_c(nt - 1, c)
    T_store(nt - 1)
    P.finish()


def norm_mod_h(P, G, X, xsq, ps_ss, rstd, tmp, h, n, l, sub, j, b_x, b_xsq, b_ss, b_rstd, b_tmp, b_h):
    ones, A, B = G["ones"], G["A"], G["B"]
    P.op("act", lambda e: e.activation(xsq[:, :, :n], X[:, :, :n], AF.Square), reads=[b_x], writes=[b_xsq])
    for kc in range(8):
        P.mm(ps_ss[:, :n], ones[:], xsq[:, kc, :n], start=(kc == 0), stop=(kc == 7), reads=[b_xsq], writes=[b_ss])
    rms_rstd(P, ps_ss, rstd, n, b_ss, b_rstd, G["epsb"])
    for kc in range(8):
        k2 = kc % 2
        P.op("dve", lambda e, kc=kc, k2=k2: e.tensor_tensor(tmp[k2][:, :n], X[:, kc, :n], rstd[:, :n], ALU.mult),
             reads=[b_x, b_rstd], writes=[b_tmp[k2]])
        P.op("act", lambda e, kc=kc, k2=k2: e.activation(
            h[:, kc, :n], tmp[k2][:, :n], AF.Identity, bias=B[:, l, sub, j, kc:kc + 1], scale=A[:, l, sub, j, kc:kc + 1]),
            reads=[b_tmp[k2], G["b"]], writes=[b_h])


def proj_phase(nc, G, dr, l, cfg):
    P = Phase(nc, f"proj{l}")
    N = 512
    XT, QK, VT = dr["XT"], dr["QK"], dr["VT"]
    wf = P.sb("wf", [128, 8, 1536], BF16)
    wp = P.sb("wp", [128, 8, 512], BF16)
    wv = P.sb("wv", [128, 8, 512], BF16)
    xt = [P.sb(f"xt{i}", [128, 8, N], F32) for i in range(2)]
    xsq2 = [P.sb(f"xsq{i}", [128, 8, N], BF16) for i in range(2)]
    tmp = [P.sb(f"tmp{i}", [128, N], F32) for i in range(2)]
    h2 = [P.sb(f"h{i}", [128, 8, N], BF16) for i in range(2)]
    rstd = P.sb("rstd", [128, N], F32)
    cs = [P.sb(f"cs{i}", [128, 2, N], F32) for i in range(2)]
    stg = [P.sb(f"stg{i}", [128, 12, N], BF16) for i in range(2)]
    vst = [P.sb(f"vst{i}", [128, 4, 512], BF16) for i in range(2)]
    t1 = [P.sb(f"t1{i}", [128, N], F32) for i in range(2)]
    t2 = [P.sb(f"t2{i}", [128, N], F32) for i in range(2)]
    ps_ss = P.ps("ss")
    ps_f = [P.ps(f"f{i}") for i in range(3)]
    ps_p = [P.ps(f"p{i}") for i in range(2)]
    ps_v = [P.ps(f"v{i}") for i in range(2)]
    b_w = P.buf()
    b_xt = [P.buf(), P.buf()]
    b_cs = [P.buf(), P.buf()]
    b_stg = [P.buf(), P.buf()]
    b_vst = [P.buf(), P.buf()]
    b_rstd, b_ss = P.buf(), P.buf()
    b_xsq2 = [P.buf(), P.buf()]
    b_h2 = [P.buf(), P.buf()]
    b_tmp = [P.buf(), P.buf()]
    b_t1 = [P.buf(), P.buf()]
    b_t2 = [P.buf(), P.buf()]
    b_pf = [P.buf() for _ in range(3)]
    b_pp = [P.buf() for _ in range(2)]
    b_pv = [P.buf() for _ in range(2)]
    s_w = P.dsem_sw("w")
    s_ld = [P.dsem("ld0"), P.dsem("ld1")]
    s_cs = [P.dsem("cs0"), P.dsem("cs1")]
    s_st = [P.dsem("st0"), P.dsem("st1")]
    s_sv = [P.dsem("sv0"), P.dsem("sv1")]
    P.track(G["b"])
    wfsrc = dr["w_in_fm"][l].rearrange("(kc p) n -> p kc n", p=128)
    b_wg = [P.buf() for _ in range(4)]
    s_wg = [P.dsem_sw(f"wg{i}") for i in range(4)]
    for gi in range(4):
        P.dma("pool", wf[:, :, gi * 384:(gi + 1) * 384], wfsrc[:, :, gi * 384:(gi + 1) * 384], s_wg[gi], writes=[b_wg[gi]])
    P.dma("pool", wp[:], dr["w_in_pp"][l].rearrange("(kc p) n -> p kc n", p=128), s_w, writes=[b_w])
    P.dma("pool", wv[:], dr["w_in_v"][l].rearrange("(kc p) n -> p kc n", p=128), s_w, writes=[b_w])
    tiles = token_tiles(N)
    if cfg.proj_tiles is not None:
        tiles = tiles[:cfg.proj_tiles]
    def loads(ti):
        t0, n, j = tiles[ti]
        sl = ti % 2
        P.dma("sp", xt[sl][:, :, :n], XT[:, t0:t0 + n].rearrange("(kc p) t -> p kc t", p=128), s_ld[sl], writes=[b_xt[sl]])

    def load_cs(ti):
        t0, n, j = tiles[ti]
        sl = ti % 2
        if j == 0:
            P.dma("sp", cs[sl][:, :, :n], dr["rope_cs"][:, :, t0:t0 + n].rearrange("c p t -> p c t"), s_cs[sl], writes=[b_cs[sl]])

    def Hst(ti):
        t0, n, j = tiles[ti]
        sl = ti % 2
        norm_mod_h(P, G, xt[sl], xsq2[sl], ps_ss, rstd, tmp, h2[sl], n, l, 1, j, b_xt[sl], b_xsq2[sl], b_ss, b_rstd, b_tmp, b_h2[sl])

    loads(0)
    load_cs(0)
    Hst(0)
    if len(tiles) > 1:
        loads(1)
    for ti, (t0, n, j) in enumerate(tiles):
        sl = ti % 2
        h = h2[sl]
        b_h = b_h2[sl]
        if ti + 1 < len(tiles):
            load_cs(ti + 1)
            Hst(ti + 1)
        if ti + 2 < len(tiles):
            loads(ti + 2)
        for ch in range(12):
            r = ch % 3
            for kc in range(8):
                P.mm(ps_f[r][:, :n], wf[:, kc, ch * 128:(ch + 1) * 128], h[:, kc, :n], start=(kc == 0), stop=(kc == 7),
                     reads=[b_wg[ch // 3], b_h], writes=[b_pf[r]])
            if 6 <= ch <= 9 and j == 0:
                r2 = ch % 2
                for kc in range(8):
                    P.mm(ps_p[r2][:, :n], wp[:, kc, (ch - 6) * 128:(ch - 5) * 128], h[:, kc, :n], start=(kc == 0), stop=(kc == 7),
                         reads=[b_w, b_h], writes=[b_pp[r2]])
                P.op("dve", lambda e, r=r, r2=r2, sl=sl, n=n: e.tensor_tensor(t1[r2][:, :n], ps_f[r][:, :n], cs[sl][:, 0, :n], ALU.mult),
                     reads=[b_pf[r], b_cs[sl]], writes=[b_t1[r2]])
                P.op("dve", lambda e, r2=r2, sl=sl, n=n: e.tensor_tensor(t2[r2][:, :n], ps_p[r2][:, :n], cs[sl][:, 1, :n], ALU.mult),
                     reads=[b_pp[r2], b_cs[sl]], writes=[b_t2[r2]])
                P.op("pool", lambda e, r2=r2, sl=sl, ch=ch, n=n: e.tensor_tensor(stg[sl][:, ch, :n], t1[r2][:, :n], t2[r2][:, :n], ALU.add),
                     reads=[b_t1[r2], b_t2[r2]], writes=[b_stg[sl]])
            else:
                P.op("act", lambda e, r=r, sl=sl, ch=ch, n=n: e.activation(stg[sl][:, ch, :n], ps_f[r][:, :n], AF.Identity),
                     reads=[b_pf[r]], writes=[b_stg[sl]])
        P.dma("sp", QK[:, :, t0:t0 + n].rearrange("c p t -> p c t"), stg[sl][:, :, :n], s_st[sl], reads=[b_stg[sl]])
        nst = n // 128
        for st in range(nst):
            r = st % 2
            for kc in range(8):
                P.mm(ps_v[r][:, :], h[:, kc, st * 128:(st + 1) * 128], wv[:, kc, :], start=(kc == 0), stop=(kc == 7),
                     reads=[b_w, b_h], writes=[b_pv[r]])
            P.op("act", lambda e, r=r, sl=sl, st=st: e.activation(vst[sl][:, st, :], ps_v[r][:, :], AF.Identity),
                 reads=[b_pv[r]], writes=[b_vst[sl]])
        tc0 = t0 // 128
        P.dma("sp", VT[tc0:tc0 + nst].rearrange("c p f -> p c f"), vst[sl][:, :nst, :], s_sv[sl], reads=[b_vst[sl]])
    P.finish()


def na_chunks(t):
    if t >= 32:
        return [], 0
    if t == 0:
        return [0, 1, 2, 3], 3
    if t == 1:
        return [0, 1, 2, 3], 2
    if t == 30:
        return [28, 29, 30, 31], 1
    if t == 31:
        return [28, 29, 30, 31], 0
    return [t - 2, t - 1, t, t + 1, t + 2], 7


def attnA_phase(nc, G, dr, l, cfg):
    P = Phase(nc, f"attA{l}")
    QK, VT, OT = dr["QK"], dr["VT"], dr["OT"]
    nqt = 34 if l < DEPTH - 1 else 32
    if cfg.att_tiles is not None:
        nqt = cfg.att_tiles
    ones = G["ones"]
    QTz = [P.sb(f"qtz{i}", [128, 2, NTOK], BF16) for i in range(2)]
    KT = [P.sb(f"kt{i}", [128, NTOK], BF16) for i in range(2)]
    V = [P.sb(f"v{i}", [128, 34, 128], BF16) for i in range(2)]
    bias = [P.sb(f"bias{i}", [128, 12, 256], F32) for i in range(2)]
    ost = [P.sb(f"ost{i}", [128, NTOK], BF16) for i in range(2)]
    ebias = [P.sb(f"ebias{i}", [128, 12, 256], BF16) for i in range(2)]
    b_eb = [P.buf(), P.buf()]
    El = [P.sb(f"el{i}", [128, 1280], BF16) for i in range(3)]
    Ec = [P.sb(f"ec{i}", [128, 512], BF16) for i in range(3)]
    rd = P.sb("rd", [128, 256], F32)
    ps_sl = P.ps("sl", [128, 1536])
    ps_sc = P.ps("sc")
    ps_o = [P.ps("o0"), P.ps("o1")]
    ps_d = [P.ps("d0"), P.ps("d1")]
    b_in = [P.buf(), P.buf()]
    b_ost = [P.buf(), P.buf()]
    b_sl = [P.buf(), P.buf()]
    b_El = [P.buf() for _ in range(3)]
    b_Ec = [P.buf() for _ in range(3)]
    b_psl, b_psc, b_rd = P.buf(), P.buf(), P.buf()
    b_po = [P.buf(), P.buf()]
    b_pd = [P.buf(), P.buf()]
    s_in = [P.dsem("in0"), P.dsem("in1")]
    s_out = [P.dsem("out0"), P.dsem("out1")]
    npairs = 3 if cfg.att_pairs is None else cfg.att_pairs
    for i in range(2):
        P.op("dve", lambda e, i=i: e.memset(QTz[i][:], 0.0), writes=[b_in[i]])

    def pair_loads(hp):
        s = hp % 2
        P.dma("sp", QTz[s][0:64, 0, :], QK[hp][0:64, :], s_in[s], writes=[b_in[s]])
        P.dma("sp", QTz[s][64:128, 1, :], QK[hp][64:128, :], s_in[s], writes=[b_in[s]])
        P.dma("sp", KT[s][:], QK[3 + hp], s_in[s], writes=[b_in[s]])
        P.dma("sp", V[s][:], VT[:, :, hp * 128:(hp + 1) * 128].rearrange("c p f -> p c f"), s_in[s], writes=[b_in[s]])
        P.dma("sp", bias[s][:], dr["na_bias"][l][:, hp].rearrange("p c e q -> p c (e q)"), s_in[s], writes=[b_in[s]])

    def pair_ebias(hp):
        s = hp % 2
        P.op("act", lambda e: e.activation(ebias[s][:], bias[s][:], AF.Exp), reads=[b_in[s]], writes=[b_eb[s]])

    pair_loads(0)
    pair_ebias(0)
    for hp in range(npairs):
        s = hp % 2
        if hp + 1 < npairs:
            pair_loads(hp + 1)

        def s_stage(t, s=s):
            r, r3 = t % 2, t % 3
            chunks, i0 = na_chunks(t)
            nl = len(chunks)
            q_ap = QTz[s][:, :, t * 128:(t + 1) * 128]
            for ii, kc in enumerate(chunks):
                P.mm(ps_sl[:, ii * 256:(ii + 1) * 256].rearrange("p (e q) -> p e q", e=2), KT[s][:, kc * 128:(kc + 1) * 128], q_ap,
                     start=True, stop=True, reads=[b_in[s]], writes=[b_psl])
            for ii, kc in enumerate((32, 33)):
                P.mm(ps_sc[:, ii * 256:(ii + 1) * 256].rearrange("p (e q) -> p e q", e=2), KT[s][:, kc * 128:(kc + 1) * 128], q_ap,
                     start=True, stop=True, reads=[b_in[s]], writes=[b_psc])
            if nl:
                P.op("act", lambda en: en.activation(El[r3][:, :nl * 256], ps_sl[:, :nl * 256], AF.Exp, scale=0.125),
                     reads=[b_psl], writes=[b_El[r3]])
                P.op("dve", lambda en: en.tensor_tensor(
                    El[r3][:, :nl * 256], El[r3][:, :nl * 256], ebias[s][:, i0:i0 + nl, :].rearrange("p a b -> p (a b)"), ALU.mult),
                    reads=[b_El[r3], b_eb[s]], writes=[b_El[r3]])
            P.op("act", lambda en: en.activation(Ec[r3][:, :], ps_sc[:, :512], AF.Exp, scale=0.125),
                 reads=[b_psc], writes=[b_Ec[r3]])

        def pv_stage(t, s=s):
            r3 = t % 3
            ob = t % 2
            chunks, i0 = na_chunks(t)
            srcs = [(El[r3][:, ii * 256:(ii + 1) * 256], kc, b_El[r3]) for ii, kc in enumerate(chunks)]
            srcs += [(Ec[r3][:, ii * 256:(ii + 1) * 256], kc, b_Ec[r3]) for ii, kc in enumerate((32, 33))]
            for idx, (ap, kc, bb) in enumerate(srcs):
                P.mm(ps_o[ob][:, 0:256], V[s][:, kc, :], ap, start=(idx == 0), stop=(idx == len(srcs) - 1),
                     reads=[bb, b_in[s]], writes=[b_po[ob]])
            for idx, (ap, kc, bb) in enumerate(srcs):
                P.mm(ps_d[ob][:, 0:256], ones[:, :], ap, start=(idx == 0), stop=(idx == len(srcs) - 1),
                     reads=[bb], writes=[b_pd[ob]])
            P.op("act", lambda en: en.activation(rd[:, :], ps_d[ob][:, 0:256], AF.Ln), reads=[b_pd[ob]], writes=[b_rd])
            P.op("act", lambda en: en.activation(rd[:, :], rd[:, :], AF.Exp, scale=-1.0), reads=[b_rd], writes=[b_rd])
            P.op("dve", lambda en: en.tensor_tensor(ost[s][0:64, t * 128:(t + 1) * 128], ps_o[ob][0:64, 0:128], rd[0:64, 0:128], ALU.mult),
                 reads=[b_po[ob], b_rd], writes=[b_ost[s]])
            P.op("dve", lambda en: en.tensor_tensor(ost[s][64:128, t * 128:(t + 1) * 128], ps_o[ob][64:128, 128:256], rd[64:128, 128:256], ALU.mult),
                 reads=[b_po[ob], b_rd], writes=[b_ost[s]])

        for t in range(nqt + 1):
            if t < nqt:
                s_stage(t)
            if t >= 1:
                pv_stage(t - 1)
            if t == 12 and hp + 1 < npairs:
                pair_ebias(hp + 1)
        P.dma("sp", OT[hp][:, :nqt * 128], ost[s][:, :nqt * 128], s_out[s], reads=[b_ost[s]])
    P.finish()


def attnB_phase(nc, G, dr, l, cfg):
    P = Phase(nc, f"attB{l}")
    QK, VT, OT = dr["QK"], dr["VT"], dr["OT"]
    nqb = 34 if l < DEPTH - 1 else 32
    if cfg.att_tiles is not None:
        nqb = cfg.att_tiles
    QTz = P.sb("qtz", [128, 2, 3, NTOK], BF16)
    KT = P.sb("kt", [128, NTOK], BF16)
    Va = [P.sb(f"va{i}", [128, 34, 128], BF16) for i in range(2)]
    msk = P.sb("msk", [128, 2, 384], F32)
    sink = P.sb("sink", [128, 6], F32)
    esink = P.sb("esink", [128, 6], F32)
    swp = P.sb("swp", [128, 128], F32)
    ost = P.sb("ost", [128, 3, NTOK], BF16)
    sbs = [P.sb(f"sbs{i}", [128, 384], F32) for i in range(2)]
    E = [P.sb(f"e{i}", [128, 384], BF16) for i in range(6)]
    Dn2 = [P.sb(f"dn{i}", [128, 384], F32) for i in range(2)]
    rd2 = [P.sb(f"rd{i}", [128, 384], F32) for i in range(2)]
    esf = P.sb("esf", [128, 384], F32)
    ps_s = [P.ps(f"s{i}") for i in range(3)]
    ps_A = [P.ps("A0"), P.ps("A1")]
    ps_B = [P.ps("B0"), P.ps("B1")]
    ps_w = P.ps("w")
    b_in, b_sink, b_esink, b_ost, b_pw, b_esf = (P.buf() for _ in range(6))
    b_rd2 = [P.buf(), P.buf()]
    b_dn2 = [P.buf(), P.buf()]
    b_va = [P.buf(), P.buf()]
    b_sbs = [P.buf(), P.buf()]
    b_E = [P.buf() for _ in range(6)]
    b_ps = [P.buf() for _ in range(3)]
    b_pA = [P.buf(), P.buf()]
    b_pB = [P.buf(), P.buf()]
    s_in = P.dsem("in")
    s_v = P.dsem("v")
    s_out = P.dsem("out")
    P.op("dve", lambda e: e.memset(QTz[:], 0.0), writes=[b_in])
    for i in range(3):
        P.dma("sp", QTz[0:64, 0, i, :], QK[6 + i][0:64, :], s_in, writes=[b_in])
        P.dma("sp", QTz[64:128, 1, i, :], QK[6 + i][64:128, :], s_in, writes=[b_in])
    P.dma("sp", KT[:], QK[9], s_in, writes=[b_in])
    s_v2 = [s_v, P.dsem("v1")]
    for i in range(2):
        P.dma("sp", Va[i][:], VT[:, :, 384:512].rearrange("c p f -> p c f"), s_v2[i], writes=[b_va[i]])
    P.dma("sp", msk[:], dr["swa_mask"][:], s_in, writes=[b_in])
    P.dma("sp", swp[:], dr["swapm"][:], s_in, writes=[b_in])
    P.dma("sp", sink[:], dr["sinkb"][l], P.dsem("sink"), writes=[b_sink])
    P.op("act", lambda e: e.activation(esink[:], sink[:], AF.Exp), reads=[b_sink], writes=[b_esink])
    P.op("dve", lambda e: e.memset(esf[:], 0.0), writes=[b_esf])
    for a in range(3):
        P.op("dve", lambda e, a=a: e.tensor_scalar(esf[0:64, a * 128:(a + 1) * 128], esf[0:64, a * 128:(a + 1) * 128],
                                                     esink[0:64, 3 + a:4 + a], None, ALU.add), reads=[b_esink, b_esf], writes=[b_esf])
        P.op("dve", lambda e, a=a: e.tensor_scalar(esf[64:128, a * 128:(a + 1) * 128], esf[64:128, a * 128:(a + 1) * 128],
                                                     esink[64:128, a:a + 1], None, ALU.add), reads=[b_esink, b_esf], writes=[b_esf])
    P.op("pool", lambda e: e.memset(Va[0][:, :, 64:128], 1.0), reads=[b_va[0]], writes=[b_va[0]])
    P.op("pool", lambda e: e.memset(Va[1][:, :, 0:64], 1.0), reads=[b_va[1]], writes=[b_va[1]])
    units = []
    for n in range(nqb):
        for kv in range(2):
            if n < 32:
                cl = []
                if n - 1 >= 0:
                    cl.append((n - 1, 0))
                cl.append((n, None))
                if n + 1 < 32:
                    cl.append((n + 1, 1))
                cl += [(32, None), (33, None)]
            else:
                cl = [(32, None), (33, None)]
            for ci, (kc, mi) in enumerate(cl):
                units.append((n, kv, kc, mi, ci == 0, ci == len(cl) - 1))

    def s_stage(i):
        n, kv, kc, mi, first, last = units[i]
        r, r2, r4 = i % 3, i % 2, i % 6
        P.mm(ps_s[r][:, 0:384].rearrange("p (a q) -> p a q", a=3), KT[:, kc * 128:(kc + 1) * 128],
             QTz[:, kv, :, n * 128:(n + 1) * 128], start=True, stop=True, reads=[b_in], writes=[b_ps[r]])
        if mi is not None:
            P.op("dve", lambda en: en.scalar_tensor_tensor(sbs[r2][:, :], ps_s[r][:, 0:384], 0.125, msk[:, mi, :], ALU.mult, ALU.add),
                 reads=[b_ps[r], b_in], writes=[b_sbs[r2]])
            P.op("act", lambda en: en.activation(E[r4][:, :], sbs[r2][:, :], AF.Exp), reads=[b_sbs[r2]], writes=[b_E[r4]])
        else:
            P.op("act", lambda en: en.activation(E[r4][:, :], ps_s[r][:, 0:384], AF.Exp, scale=0.125),
                 reads=[b_ps[r]], writes=[b_E[r4]])

    def pv_stage(i):
        n, kv, kc, mi, first, last = units[i]
        r4 = i % 6
        ob = n % 2
        acc, bacc = (ps_A[ob], b_pA[ob]) if kv == 0 else (ps_B[ob], b_pB[ob])
        P.mm(acc[:, 0:384], Va[kv][:, kc, :], E[r4][:, :], start=first, stop=last, reads=[b_E[r4], b_va[kv]], writes=[bacc])
        return last and kv == 1

    def F1(n):
        ob = n % 2
        A, B_ = ps_A[ob], ps_B[ob]
        P.op("dve", lambda en: en.tensor_tensor(Dn2[ob][0:64, :], B_[0:64, 0:384], esf[0:64, :], ALU.add),
             reads=[b_pB[ob], b_esf], writes=[b_dn2[ob]])
        P.op("dve", lambda en: en.tensor_tensor(Dn2[ob][64:128, :], A[64:128, 0:384], esf[64:128, :], ALU.add),
             reads=[b_pA[ob], b_esf], writes=[b_dn2[ob]])

    def F2(n):
        ob = n % 2
        P.mm(ps_w[:, 0:384], swp[:, :], Dn2[ob][:, :], start=True, stop=True, reads=[b_dn2[ob], b_in], writes=[b_pw])
        P.op("act", lambda en: en.activation(rd2[ob][:, :], ps_w[:, 0:384], AF.Ln), reads=[b_pw], writes=[b_rd2[ob]])
        P.op("act", lambda en: en.activation(rd2[ob][:, :], rd2[ob][:, :], AF.Exp, scale=-1.0), reads=[b_rd2[ob]], writes=[b_rd2[ob]])

    def F3(n):
        ob = n % 2
        A, B_ = ps_A[ob], ps_B[ob]
        P.op("dve", lambda en: en.tensor_tensor(
            ost[0:64, :, n * 128:(n + 1) * 128], A[0:64, 0:384].rearrange("p (a q) -> p a q", a=3),
            rd2[ob][0:64, :].rearrange("p (a q) -> p a q", a=3), ALU.mult), reads=[b_pA[ob], b_rd2[ob]], writes=[b_ost])
        P.op("dve", lambda en: en.tensor_tensor(
            ost[64:128, :, n * 128:(n + 1) * 128], B_[64:128, 0:384].rearrange("p (a q) -> p a q", a=3),
            rd2[ob][64:128, :].rearrange("p (a q) -> p a q", a=3), ALU.mult), reads=[b_pB[ob], b_rd2[ob]], writes=[b_ost])

    LAG = 4
    pend = {}
    total = len(units) + LAG
    i = 0
    while i < total or pend:
        if i < len(units):
            s_stage(i)
        if LAG <= i < total:
            if pv_stage(i - LAG):
                nn = units[i - LAG][0]
                F1(nn)
                pend.setdefault(i + 2, []).append((F2, nn))
                pend.setdefault(i + 3, []).append((F3, nn))
        for fn, nn in pend.pop(i, []):
            fn(nn)
        i += 1
    for a in range(3):
        P.dma("sp", OT[3 + a][:, :nqb * 128], ost[:, a, :nqb * 128], s_out, reads=[b_ost])
    P.finish()


def fnet_phase(nc, G, dr, l, cfg):
    P = Phase(nc, f"fnet{l}")
    QK, OT = dr["QK"], dr["OT"]
    do_ctx = l < DEPTH - 1
    ntc = 34 if do_ctx else 32
    fuT = P.sb("fuT", [128, 2, NTOK], BF16)
    bd = P.sb("bd", [128, 256], BF16)
    C0 = P.sb("c0", [128, 32, 512], BF16)
    S0 = P.sb("s0", [128, 32, 512], BF16)
    cc = P.sb("cc", [128, 2, 2, 256], BF16)
    casa = P.sb("casa", [128, 2, 8], F32)
    ucs = P.sb("ucs", [128, 34, 2, 2, 128], BF16)
    PQ = [P.sb(f"pq{i}", [128, 32, 2, 128], BF16) for i in range(2)]
    tA = P.sb("tA", [128, 32, 128], BF16)
    tB = P.sb("tB", [128, 32, 128], BF16)
    ost = P.sb("ost", [128, 2, NTOK], BF16)
    ps_u = [P.ps(f"u{i}") for i in range(2)]
    ps_f = [P.ps(f"f{i}") for i in range(2)]
    b_in, b_dft, b_ucs, b_tA, b_tB, b_ost = (P.buf() for _ in range(6))
    b_PQ = [P.buf(), P.buf()]
    b_pu = [P.buf(), P.buf()]
    b_pf = [P.buf(), P.buf()]
    s_in = P.dsem("in")
    s_dft = P.dsem("dft")
    s_out = P.dsem("out")
    for fc in range(2):
        P.dma("sp", fuT[:, fc, :], QK[10 + fc], s_in, writes=[b_in])
    P.dma("sp", bd[:], dr["bd64"][:], s_in, writes=[b_in])
    P.dma("sp", casa[:], dr["casa"][:], s_in, writes=[b_in])
    P.dma("sp", cc[:], dr["dft_ctx"][:], s_in, writes=[b_in])
    P.dma("sp", C0[:], dr["dft0"][0].rearrange("(mc p) n -> p mc n", p=128), s_dft, writes=[b_dft])
    P.dma("sp", S0[:], dr["dft0"][1].rearrange("(mc p) n -> p mc n", p=128), s_dft, writes=[b_dft])
    k = 0
    for tc in range(ntc):
        for fc in range(2):
            r = k % 2
            k += 1
            P.mm(ps_u[r][:, 0:256], fuT[:, fc, tc * 128:(tc + 1) * 128], bd[:, :], start=True, stop=True, reads=[b_in], writes=[b_pu[r]])
            P.op("act", lambda e, r=r, tc=tc, fc=fc: e.activation(
                ucs[:, tc, fc, :, :].rearrange("p a b -> p (a b)"), ps_u[r][:, 0:256], AF.Identity), reads=[b_pu[r]], writes=[b_ucs])
    nts = 8 if cfg.fnet_tiles is None else cfg.fnet_tiles
    iters = [(nt, fc) for nt in range(nts) for fc in range(2)]

    def rot(k):
        nt, fc = iters[k]
        r = k % 2
        uc = ucs[:, 0:32, fc, 0, :]
        us = ucs[:, 0:32, fc, 1, :]
        if nt == 0:
            return uc, us, b_ucs
        ca = casa[:, 0, nt:nt + 1]
        sa = casa[:, 1, nt:nt + 1]
        P.op("act", lambda e: e.activation(tA[:], us, AF.Identity, scale=sa), reads=[b_ucs, b_in], writes=[b_tA])
        P.op("dve", lambda e: e.scalar_tensor_tensor(PQ[r][:, :, 0, :], uc, ca, tA[:], ALU.mult, ALU.subtract),
             reads=[b_ucs, b_tA, b_in], writes=[b_PQ[r]])
        P.op("act", lambda e: e.activation(tB[:], uc, AF.Identity, scale=sa), reads=[b_ucs, b_in], writes=[b_tB])
        P.op("dve", lambda e: e.scalar_tensor_tensor(PQ[r][:, :, 1, :], us, ca, tB[:], ALU.mult, ALU.add),
             reads=[b_ucs, b_tB, b_in], writes=[b_PQ[r]])
        return PQ[r][:, :, 0, :], PQ[r][:, :, 1, :], b_PQ[r]

    def mmk(k, ops):
        nt, fc = iters[k]
        r = k % 2
        Pm, Qm, bb = ops
        for mc in range(32):
            P.mm(ps_f[r][:, :], Pm[:, mc, :], C0[:, mc, :], start=(mc == 0), stop=False, reads=[bb, b_dft], writes=[b_pf[r]])
            P.mm(ps_f[r][:, :], Qm[:, mc, :], S0[:, mc, :], start=False, stop=(mc == 31), reads=[bb, b_dft], writes=[b_pf[r]])
        P.op("act", lambda e: e.activation(ost[:, fc, nt * 512:(nt + 1) * 512], ps_f[r][:, :], AF.Identity, scale=1.0 / 512),
             reads=[b_pf[r]], writes=[b_ost])

    nxt_ops = rot(0)
    for k in range(len(iters)):
        cur = nxt_ops
        if k + 1 < len(iters):
            nxt_ops = rot(k + 1)
        mmk(k, cur)
    k = len(iters)
    if do_ctx:
        for fc in range(2):
            r = k % 2
            k += 1
            for mc in range(2):
                P.mm(ps_f[r][:, 0:256], ucs[:, 32 + mc, fc, 0, :], cc[:, mc, 0, :], start=(mc == 0), stop=False, reads=[b_ucs, b_in], writes=[b_pf[r]])
                P.mm(ps_f[r][:, 0:256], ucs[:, 32 + mc, fc, 1, :], cc[:, mc, 1, :], start=False, stop=(mc == 1), reads=[b_ucs, b_in], writes=[b_pf[r]])
            P.op("act", lambda e, r=r, fc=fc: e.activation(ost[:, fc, S:NTOK], ps_f[r][:, 0:256], AF.Identity, scale=1.0 / 128),
                 reads=[b_pf[r]], writes=[b_ost])
    for fc in range(2):
        P.dma("sp", OT[6 + fc][:, :ntc * 128], ost[:, fc, :ntc * 128], s_out, reads=[b_ost])
    P.finish()


def outproj_phase(nc, G, dr, l, cfg):
    P = Phase(nc, f"oproj{l}")
    N = 512
    sub = 1
    XT, OT = dr["XT"], dr["OT"]
    ones, Gt = G["ones"], G["Gt"]
    wo = P.sb("wo", [128, 8, D], BF16)
    NS = 3
    xt = [P.sb(f"xt{i}", [128, 8, N], F32) for i in range(NS)]
    ot = [P.sb(f"ot{i}", [128, 8, N], BF16) for i in range(NS)]
    xsq = P.sb("xsq", [128, 8, N], BF16)
    yb = [P.sb(f"y{i}", [128, 8, N], F32) for i in range(2)]
    rstd2 = P.sb("rstd2", [128, N], F32)
    tmp = [P.sb(f"tmp{i}", [128, N], F32) for i in range(2)]
    ps_ss = P.ps("ss")
    ps_y = [P.ps(f"y{i}") for i in range(2)]
    b_w = P.buf()
    b_xt = [[P.buf() for _ in range(8)] for _ in range(NS)]
    b_ot = [P.buf() for _ in range(NS)]
    b_xsq, b_rstd2, b_ss = (P.buf() for _ in range(3))
    b_yb = [[P.buf() for _ in range(8)] for _ in range(2)]
    b_tmp = [P.buf(), P.buf()]
    b_py = [P.buf(), P.buf()]
    s_w = P.dsem_sw("w")
    s_ld = [P.dsem(f"ld{i}") for i in range(NS)]
    s_ldo = [P.dsem(f"ldo{i}") for i in range(NS)]
    s_st = [P.dsem(f"st{i}") for i in range(NS)]
    P.track(G["b"])
    P.dma("pool", wo[:], dr["w_out_p"][l].rearrange("(kc p) n -> p kc n", p=128), s_w, writes=[b_w])
    tiles = token_tiles(N)
    if l == DEPTH - 1:
        tiles = [t for t in tiles if t[2] == 0]
    if cfg.proj_tiles is not None:
        tiles = tiles[:cfg.proj_tiles]
    def loads(ti):
        t0, n, j = tiles[ti]
        sl = ti % NS
        P.dma("sp", xt[sl][:, :, :n], XT[:, t0:t0 + n].rearrange("(kc p) t -> p kc t", p=128), s_ld[sl], writes=b_xt[sl])
        P.dma("sp", ot[sl][:, :, :n], OT[:, :, t0:t0 + n].rearrange("c p t -> p c t"), s_ldo[sl], writes=[b_ot[sl]])

    loads(0)
    if len(tiles) > 1:
        loads(1)
    for ti, (t0, n, j) in enumerate(tiles):
        sl = ti % NS
        X = xt[sl]
        y = yb[ti % 2]
        b_y = b_yb[ti % 2]
        if ti + 2 < len(tiles):
            loads(ti + 2)
        for c in range(8):
            pb = c % 2
            for kc in range(8):
                P.mm(ps_y[pb][:, :n], wo[:, kc, c * 128:(c + 1) * 128], ot[sl][:, kc, :n], start=(kc == 0), stop=(kc == 7),
                     reads=[b_w, b_ot[sl]], writes=[b_py[pb]])
            P.op("act", lambda e, pb=pb, c=c, n=n: e.activation(xsq[:, c, :n], ps_y[pb][:, :n], AF.Square), reads=[b_py[pb]], writes=[b_xsq])
            P.op("act", lambda e, pb=pb, c=c, j=j, n=n, y=y: e.activation(y[:, c, :n], ps_y[pb][:, :n], AF.Identity, scale=Gt[:, l, sub, j, c:c + 1]),
                 reads=[b_py[pb], G["b"]], writes=[b_y[c]])
        for c in range(8):
            P.mm(ps_ss[:, :n], ones[:], xsq[:, c, :n], start=(c == 0), stop=(c == 7), reads=[b_xsq], writes=[b_ss])
        rms_rstd(P, ps_ss, rstd2, n, b_ss, b_rstd2, G["epsb"])
        for c in range(8):
            k2 = c % 2
            P.op("dve", lambda e, c=c, n=n, y=y: e.tensor_tensor(y[:, c, :n], y[:, c, :n], rstd2[:, :n], ALU.mult),
                 reads=[b_y[c], b_rstd2], writes=[b_y[c]])
            P.op("pool", lambda e, X=X, c=c, n=n, y=y: e.tensor_tensor(X[:, c, :n], X[:, c, :n], y[:, c, :n], ALU.add),
                 reads=[b_y[c], b_xt[sl][c]], writes=[b_xt[sl][c]])
        P.dma("sp", XT[:, t0:t0 + n].rearrange("(kc p) t -> p kc t", p=128), X[:, :, :n], s_st[sl], reads=b_xt[sl])
    P.finish()


def build(cfg):
    nc = bass.Bass("TRN2", target_bir_lowering=False)
    _POOL[0] = SemPool(nc)
    allh = [x.h for x in _POOL[0].eng.values()] + [x.h for x in _POOL[0].dma] + [x.h for x in _POOL[0].swdma]
    for hsem in allh:
        nc.gpsimd.sem_clear(hsem)
    nc.all_engine_barrier()
    dr = {}

    def din(name, shape, dt=F32):
        dr[name] = nc.dram_tensor(name, list(shape), dt, kind="ExternalInput").ap()

    din("XTin", [D, NTOK])
    din("cin", [128, 8, 2])
    din("bmod", [DEPTH, 128, 72])
    din("gpp", [DEPTH, 128, 2, 3, 8])
    din("w_mod", [DEPTH, D, 9 * D])
    din("w_ffn_in", [DEPTH, 2, D, 2 * DFF])
    din("w_ffn_out", [DEPTH, 2, DFF, D])
    din("w_in_fm", [DEPTH, D, 1536])
    din("w_in_pp", [DEPTH, D, 512])
    din("w_in_v", [DEPTH, D, 512])
    din("w_out_p", [DEPTH, D, D])
    din("rope_cs", [2, 128, S])
    din("na_bias", [DEPTH, 128, 3, 12, 2, 128])
    din("swa_mask", [128, 2, 384])
    din("sinkb", [DEPTH, 128, 6])
    din("swapm", [128, 128])
    din("bd64", [128, 256], BF16)
    din("casa", [128, 2, 8])
    din("dft_ctx", [128, 2, 2, 256], BF16)
    din("dft0", [2, S, 512], BF16)
    dr["XT"] = nc.dram_tensor("XTo", [D, NTOK], F32, kind="ExternalOutput").ap()
    skind = "ExternalOutput" if cfg.debug else "Internal"
    dr["QK"] = nc.dram_tensor("QK", [12, 128, NTOK], BF16, kind=skind).ap()
    dr["VT"] = nc.dram_tensor("VT", [34, 128, 512], BF16, kind=skind).ap()
    dr["OT"] = nc.dram_tensor("OT", [8, 128, NTOK], BF16, kind=skind).ap()

    gs = ExitStack()
    G = {}
    G["A"] = gs.enter_context(nc.sbuf_tensor("gA", [128, DEPTH, 3, 2, 8], F32))
    G["B"] = gs.enter_context(nc.sbuf_tensor("gB", [128, DEPTH, 3, 2, 8], F32))
    G["Gt"] = gs.enter_context(nc.sbuf_tensor("gG", [128, DEPTH, 3, 2, 8], F32))
    G["ones"] = gs.enter_context(nc.sbuf_tensor("ones", [128, 128], BF16))
    G["epsb"] = gs.enter_context(nc.sbuf_tensor("epsb", [128, 1], F32))
    G["b"] = Buf("modvec")

    P = Phase(nc, "init")
    s0 = P.dsem("cp")
    if cfg.phases is not None:
        for i in range(8):
            P.dma("sp", dr["XT"][i * 128:(i + 1) * 128, :], dr["XTin"][i * 128:(i + 1) * 128, :], s0)
    P.op("dve", lambda e: e.memset(G["ones"][:], 1.0))
    P.op("dve", lambda e: e.memset(G["epsb"][:], float(D * EPS)))
    P.finish()

    ph = cfg.phases
    mod_phase(nc, G, dr)
    for l in range(cfg.depth):
        last = l == DEPTH - 1
        if ph is None or "ffn1" in ph:
            ffn_phase(nc, G, dr, l, 0, cfg, skip_ctx=False)
        if ph is None or "proj" in ph:
            proj_phase(nc, G, dr, l, cfg)
        if ph is None or "attA" in ph:
            attnA_phase(nc, G, dr, l, cfg)
        if ph is None or "attB" in ph:
            attnB_phase(nc, G, dr, l, cfg)
        if ph is None or "fnet" in ph:
            fnet_phase(nc, G, dr, l, cfg)
        if ph is None or "oproj" in ph:
            outproj_phase(nc, G, dr, l, cfg)
        if ph is None or "ffn2" in ph:
            ffn_phase(nc, G, dr, l, 1, cfg, skip_ctx=last)
    for hsem in allh:
        nc.gpsimd.sem_clear(hsem)
    nc.all_engine_barrier()
    gs.close()
    return nc


def _consts():
    inv = (10000.0 ** (-np.arange(16, dtype=np.float64) / 16)).astype(np.float32).astype(np.float64)
    t = np.arange(S)
    rows = (t // 64).astype(np.float64)
    cols = (t % 64).astype(np.float64)
    cos = np.zeros((64, S))
    sin = np.zeros((64, S))
    for d in range(64):
        pos = rows if d < 32 else cols
        ang = pos * inv[d % 16]
        cos[d] = np.cos(ang)
        sin[d] = np.sin(ang) * (-1.0 if (d % 32) < 16 else 1.0)
    rope = np.stack([np.concatenate([cos, cos], 0), np.concatenate([sin, sin], 0)], 0).astype(np.float32)
    k = np.arange(128)[:, None]
    q = np.arange(128)[None, :]
    m0 = np.where(k >= q, 0.0, NEG)
    m1 = np.where(k <= q, 0.0, NEG)
    swa = np.stack([np.tile(m0, (1, 3)), np.tile(m1, (1, 3))], 1).astype(np.float32)
    kk = np.arange(64)
    c64 = np.cos(2 * np.pi * np.outer(kk, kk) / 64)
    s64 = np.sin(2 * np.pi * np.outer(kk, kk) / 64)
    bd = np.zeros((128, 256))
    for g in range(2):
        bd[g * 64:(g + 1) * 64, g * 64:(g + 1) * 64] = c64
        bd[g * 64:(g + 1) * 64, 128 + g * 64:128 + (g + 1) * 64] = s64
    p = np.arange(128)
    nt = np.arange(8)
    ang = 2 * np.pi * np.outer(p % 8, nt) / 8
    casa = np.stack([np.cos(ang), np.sin(ang)], 1).astype(np.float32)
    m = np.arange(256)
    angc = 2 * np.pi * np.outer(m, m) / 256
    cc = np.stack([np.cos(angc), -np.sin(angc)], 1)
    cc = cc.reshape(2, 128, 2, 256).transpose(1, 0, 2, 3)
    mm = np.arange(S)
    a0 = 2 * np.pi * np.outer(mm, np.arange(512)) / S
    dft0 = np.stack([np.cos(a0), -np.sin(a0)], 0)
    bf = ml_dtypes.bfloat16
    swapm = np.roll(np.eye(128, dtype=np.float32), 64, axis=1)
    return dict(swapm=swapm, rope_cs=rope, swa_mask=swa, bd64=bd.astype(bf), casa=casa, dft_ctx=np.ascontiguousarray(cc).astype(bf),
                dft0=dft0.astype(bf))


def _na_bias_index():
    types = [(-6, 0), (-4, 0), (-2, 0), (0, 0), (2, 0), (4, 0), (6, 0), (-4, 1), (-2, 0), (0, 0), (2, 0), (4, 1)]
    a = (np.arange(128) // 64)[:, None]
    kc = (np.arange(128) % 64)[:, None]
    b = (np.arange(128) // 64)[None, :]
    qc = (np.arange(128) % 64)[None, :]
    ws = np.clip(qc - 8, 0, 48)
    colv = (kc >= ws) & (kc < ws + 16)
    ri = np.zeros((12, 128, 128), np.int64)
    ci = np.zeros((12, 128, 128), np.int64)
    va = np.zeros((12, 128, 128), bool)
    for i, (e, msk) in enumerate(types):
        drr = e + a - b
        rowv = np.ones((128, 128), bool)
        if msk and e == -4:
            rowv = a >= b
        if msk and e == 4:
            rowv = (a + 1) <= b
        v = colv & rowv & (np.abs(drr) <= 7)
        ri[i] = np.clip(drr + 7, 0, 14)
        ci[i] = np.clip(kc - qc + 15, 0, 30)
        va[i] = v
    return ri, ci, va


_CONST_CACHE = {}


def host_shared(inputs):
    if "c" not in _CONST_CACHE:
        _CONST_CACHE["c"] = _consts()
        _CONST_CACHE["nb"] = _na_bias_index()
    m = dict(_CONST_CACHE["c"])
    w_in = inputs["w_in"]
    hperm = np.array([i * 1 for i in range(64)])
    part = np.concatenate([np.arange(16, 32), np.arange(0, 16), np.arange(48, 64), np.arange(32, 48)])
    bq0 = 1152
    bk0 = 1536
    bq_cols = np.concatenate([np.concatenate([bq0 + i * 64 + hperm, bq0 + (3 + i) * 64 + hperm]) for i in range(3)])
    bq_pcols = np.concatenate([np.concatenate([bq0 + i * 64 + part, bq0 + (3 + i) * 64 + part]) for i in range(3)])
    bk_cols = bk0 + np.arange(128)
    bk_pcols = np.concatenate([bk0 + part, bk0 + 64 + part])
    fm_cols = np.concatenate([np.arange(0, 768), bq_cols, bk_cols, np.arange(1792, 2048)])
    pp_cols = np.concatenate([bq_pcols, bk_pcols])
    v_cols = np.concatenate([np.arange(768, 1152), np.arange(1664, 1792)])
    m["w_in_fm"] = w_in[:, :, fm_cols]
    m["w_in_pp"] = w_in[:, :, pp_cols]
    m["w_in_v"] = w_in[:, :, v_cols]
    orow = np.concatenate([np.arange(0, 384)] + [np.concatenate([384 + i * 64 + hperm, 384 + (3 + i) * 64 + hperm]) for i in range(3)]
                          + [np.arange(768, 1024)])
    m["w_out_p"] = inputs["w_out"][:, orow, :]
    ri, ci, va = _CONST_CACHE["nb"]
    rpb = inputs["na_rpb"]
    g = rpb[:, :, ri, ci]
    g = np.where(va[None, None], g, np.float32(NEG))
    g = g.reshape(DEPTH, 3, 2, 12, 128, 128)
    m["na_bias"] = g.transpose(0, 4, 1, 3, 2, 5)
    m["sinkb"] = np.broadcast_to(inputs["swa_sink"][:, None, :], (DEPTH, 128, 6))
    m["bmod"] = inputs["b_mod"].reshape(DEPTH, 72, 128).transpose(0, 2, 1)
    gpp = np.stack([inputs["g_pre"], inputs["g_post"]], axis=1)
    m["gpp"] = gpp.reshape(DEPTH, 2, 3, 8, 128).transpose(0, 4, 1, 2, 3)
    m["w_mod"] = inputs["w_mod"]
    m["w_ffn_in"] = inputs["w_ffn_in"]
    m["w_ffn_out"] = inputs["w_ffn_out"]
    out = {}
    for k2, v in m.items():
        if v.dtype == ml_dtypes.bfloat16:
            out[k2] = np.ascontiguousarray(v)
        else:
            out[k2] = np.ascontiguousarray(v, dtype=np.float32)
    return out


def host_inputs(inputs, b, shared):
    x, ctx, c, c_ctx = inputs["x"], inputs["ctx"], inputs["c"], inputs["c_ctx"]
    XT = np.ascontiguousarray(np.concatenate([x[b], ctx[b]], axis=0).T, dtype=np.float32)
    cin = np.ascontiguousarray(np.stack([c[b].reshape(8, 128).T, c_ctx.reshape(8, 128).T], axis=-1), dtype=np.float32)
    m = dict(shared)
    m["XTin"] = XT
    m["cin"] = cin
    return m


def run(inputs, cfg):
    inputs = {k: np.asarray(v) for k, v in inputs.items()}
    nc = build(cfg)
    shared = host_shared(inputs)
    in_maps = [host_inputs(inputs, b, shared) for b in range(cfg.ncores)]
    res = run_bass_kernel_spmd(nc, in_maps, core_ids=list(range(cfg.ncores)))
    return res


def kernel(**inputs):
    res = run(inputs, Cfg())
    out = np.stack([np.ascontiguousarray(r["XTo"][:, :S].T) for r in res.results], axis=0)
    return out.astype(np.float32)
```

```python
from contextlib import ExitStack
import numpy as np
import ml_dtypes
import concourse.bass as bass
import concourse.mybir as mybir
from concourse.bass_utils import run_bass_kernel_spmd

F32 = mybir.dt.float32
BF16 = mybir.dt.bfloat16
ALU = mybir.AluOpType
AF = mybir.ActivationFunctionType

D = 1024
S = 4096
CTX = 256
NTOK = S + CTX
DEPTH = 2
DFF = 2816
NJ = DFF // 128
EPS = 1e-6
NEG = -30000.0
SAME_ENG_SYNC = True


class Buf:
    __slots__ = ("name", "w", "r")

    def __init__(self, name=""):
        self.name = name
        self.w = None
        self.r = {}


class Sem:
    __slots__ = ("h", "count")

    def __init__(self, h):
        self.h = h
        self.count = 0


class SemPool:
    def __init__(self, nc, ndma=16):
        self.eng = {e: Sem(nc.alloc_semaphore(name="g_" + e)) for e in ("pe", "act", "dve", "pool")}
        self.dma = [Sem(nc.alloc_semaphore(name=f"g_dma{i}")) for i in range(ndma)]
        self.swdma = [Sem(nc.alloc_semaphore(name=f"g_swdma{i}")) for i in range(8)]


_POOL = [None]


class Phase:
    ENGS = ("pe", "act", "dve", "pool", "sp")

    def __init__(self, nc, name):
        self.nc = nc
        self.name = name
        self.es = ExitStack()
        self.q = {e: [] for e in self.ENGS}
        self.waited = {e: {} for e in self.ENGS}
        self.esem = {}
        self.dsems = []
        self.bufs = []
        self.nsem = 0
        self.allsems = []
        self.ndma = 0
        self.nsw = 0
        for e in ("pe", "act", "dve", "pool"):
            self.esem[e] = _POOL[0].eng[e]

    def sem(self, name):
        self.nsem += 1
        h = self.nc.alloc_semaphore(name=f"{self.name}_{name}_{self.nsem}")
        self.allsems.append(h)
        return Sem(h)

    def dsem(self, name="d"):
        s = _POOL[0].dma[self.ndma]
        self.ndma += 1
        self.dsems.append(s)
        return s

    def dsem_sw(self, name="d"):
        s = _POOL[0].swdma[self.nsw]
        self.nsw += 1
        self.dsems.append(s)
        return s

    def buf(self, name=""):
        b = Buf(name)
        self.bufs.append(b)
        return b

    def track(self, *bs):
        for b in bs:
            b.w = None
            b.r = {}
            self.bufs.append(b)

    def sb(self, name, shape, dt):
        return self.es.enter_context(self.nc.sbuf_tensor(f"{self.name}_{name}", list(shape), dt))

    def ps(self, name, shape=(128, 512), dt=F32):
        return self.es.enter_context(self.nc.psum_tensor(f"{self.name}_ps_{name}", list(shape), dt))

    def _waits(self, eng, reads, writes, skip=None):
        need = {}

        def add(tok):
            if tok is None:
                return
            s, v = tok
            if s is self.esem.get(eng) and (eng == "pe" or not SAME_ENG_SYNC):
                return
            if s is skip:
                return
            if need.get(s, 0) < v:
                need[s] = v

        for b in reads:
            add(b.w)
        for b in writes:
            add(b.w)
            for s, v in b.r.items():
                add((s, v))
        wl = []
        wd = self.waited[eng]
        for s, v in need.items():
            if wd.get(s, 0) < v:
                wd[s] = v
                wl.append((s.h, v))
        return wl

    @staticmethod
    def _mark(tok, reads, writes):
        s, v = tok
        for b in reads:
            if b.r.get(s, 0) < v:
                b.r[s] = v
        for b in writes:
            b.w = tok
            b.r = {}

    def op(self, eng, fn, reads=(), writes=()):
        wl = self._waits(eng, reads, writes)
        es = self.esem[eng]
        es.count += 1
        self._mark((es, es.count), reads, writes)
        h = es.h

        def thunk(e):
            for sh, v in wl:
                e.wait_ge(sh, v)
            fn(e).then_inc(h, 1)

        self.q[eng].append(thunk)

    def mm(self, out_ap, lhsT, rhs, start, stop, reads=(), writes=()):
        wl = self._waits("pe", reads, writes if start else ())
        es = self.esem["pe"]
        tok = (es, es.count + 1)
        self._mark(tok, reads, writes if stop else ())
        if not stop:
            for b in writes:
                pass
        h = es.h
        if stop:
            es.count += 1

        def thunk(e):
            for sh, v in wl:
                e.wait_ge(sh, v)
            ins = e.matmul(out_ap, lhsT, rhs, start=start, stop=stop)
            if stop:
                ins.then_inc(h, 1)

        self.q["pe"].append(thunk)

    def dma(self, q, out_ap, in_ap, sem, reads=(), writes=()):
        wl = self._waits(q, reads, writes, skip=sem)
        sem.count += 16
        self._mark((sem, sem.count), reads, writes)
        h = sem.h

        def thunk(e):
            for sh, v in wl:
                e.wait_ge(sh, v)
            e.dma_start(out=out_ap, in_=in_ap).then_inc(h, 16)

        self.q[q].append(thunk)

    def finish(self):
        fin = [(s.h, s.count) for s in self.dsems if s.count > 0]

        def thunk(e):
            for sh, v in fin:
                e.wait_ge(sh, v)

        self.q["sp"].append(thunk)
        q = self.q
        with self.nc.Block() as blk:
            @blk.tensor
            def _(e):
                for t in q["pe"]:
                    t(e)

            @blk.scalar
            def _(e):
                for t in q["act"]:
                    t(e)

            @blk.vector
            def _(e):
                for t in q["dve"]:
                    t(e)

            @blk.gpsimd
            def _(e):
                for t in q["pool"]:
                    t(e)

            @blk.sync
            def _(e):
                for t in q["sp"]:
                    t(e)
        for b in self.bufs:
            b.w = None
            b.r = {}
        self.es.close()


class Cfg:
    def __init__(self, depth=DEPTH, ffn_tiles=None, stop_after=None, ncores=8, stage=9, proj_tiles=None,
                 att_tiles=None, att_pairs=None, fnet_tiles=None, phases=None, debug=False):
        self.proj_tiles, self.att_tiles, self.att_pairs, self.fnet_tiles = proj_tiles, att_tiles, att_pairs, fnet_tiles
        self.phases = phases
        self.debug = debug
        self.stage = stage
        self.ncores = ncores
        self.depth = depth
        self.ffn_tiles = ffn_tiles
        self.stop_after = stop_after


def token_tiles(n):
    tiles = [(t0, n, 0) for t0 in range(0, S, n)]
    tiles += [(S + t0, min(n, CTX - t0), 1) for t0 in range(0, CTX, n)]
    return tiles


def mod_phase(nc, G, dr):
    for l in range(DEPTH):
        P = Phase(nc, f"mod{l}")
        P.track(G["b"])
        cin = P.sb("cin", [128, 8, 2], F32)
        cact = P.sb("cact", [128, 8, 2], BF16)
        bm = P.sb("bm", [128, 72], F32)
        gp = P.sb("gp", [128, 2, 3, 8], F32)
        modT = P.sb("modT", [128, 72, 2], F32)
        wbuf = [P.sb(f"w{i}", [128, 8, 2304], BF16) for i in range(2)]
        psm = P.ps("psm", [128, 512])
        b_cin, b_cact, b_bm, b_gp, b_mod, b_ps = (P.buf() for _ in range(6))
        b_w = [P.buf(), P.buf()]
        s_small = P.dsem("small")
        s_w = [P.dsem_sw("w0"), P.dsem_sw("w1")]
        P.dma("sp", cin[:], dr["cin"][:], s_small, writes=[b_cin])
        P.dma("sp", bm[:], dr["bmod"][l], P.dsem("bm"), writes=[b_bm])
        P.dma("sp", gp[:], dr["gpp"][l], P.dsem("gp"), writes=[b_gp])
        P.op("act", lambda e: e.activation(cact[:], cin[:], AF.Silu), reads=[b_cin], writes=[b_cact])
        wsrc = dr["w_mod"][l].rearrange("(kc p) n -> p kc n", p=128)
        for g in range(4):
            i = g % 2
            P.dma("pool", wbuf[i][:], wsrc[:, :, g * 2304:(g + 1) * 2304], s_w[i], writes=[b_w[i]])
            for nn in range(18):
                ncol = g * 18 + nn
                for kc in range(8):
                    P.mm(psm[:, 2 * ncol:2 * ncol + 2], wbuf[i][:, kc, nn * 128:(nn + 1) * 128], cact[:, kc, :],
                         start=(kc == 0), stop=(kc == 7), reads=[b_w[i], b_cact], writes=[b_ps])
        ps3 = psm[:, 0:144].rearrange("p (n j) -> p n j", j=2)
        for j in range(2):
            P.op("dve", lambda e, j=j: e.tensor_tensor(modT[:, :, j], ps3[:, :, j], bm[:], ALU.add),
                 reads=[b_ps, b_bm], writes=[b_mod])
        A, B, Gt = G["A"], G["B"], G["Gt"]
        for sub in range(3):
            for j in range(2):
                sh = modT[:, (3 * sub) * 8:(3 * sub) * 8 + 8, j]
                sc = modT[:, (3 * sub + 1) * 8:(3 * sub + 1) * 8 + 8, j]
                gt = modT[:, (3 * sub + 2) * 8:(3 * sub + 2) * 8 + 8, j]
                wgt = 32.0 * (0.5 if sub != 1 else 1.0)
                P.op("dve", lambda e, sc=sc, sub=sub, j=j: e.scalar_tensor_tensor(
                    A[:, l, sub, j, :], sc, 1.0, gp[:, 0, sub, :], ALU.add, ALU.mult), reads=[b_mod, b_gp], writes=[G["b"]])
                P.op("dve", lambda e, sub=sub, j=j: e.tensor_scalar(
                    A[:, l, sub, j, :], A[:, l, sub, j, :], 32.0, None, ALU.mult), reads=[G["b"]], writes=[G["b"]])
                P.op("dve", lambda e, sh=sh, sub=sub, j=j: e.tensor_copy(B[:, l, sub, j, :], sh),
                     reads=[b_mod], writes=[G["b"]])
                P.op("dve", lambda e, gt=gt, sub=sub, j=j, wgt=wgt: e.scalar_tensor_tensor(
                    Gt[:, l, sub, j, :], gt, wgt, gp[:, 1, sub, :], ALU.mult, ALU.mult), reads=[b_mod, b_gp], writes=[G["b"]])
        P.finish()


def rms_rstd(P, ps_ss, rstd, n, b_ps, b_rstd, epsb):
    P.op("act", lambda e: e.activation(rstd[:, :n], ps_ss[:, :n], AF.Sqrt, bias=epsb[:, 0:1], scale=1.0),
         reads=[b_ps], writes=[b_rstd])
    P.op("dve", lambda e: e.reciprocal(rstd[:, :n], rstd[:, :n]), reads=[b_rstd], writes=[b_rstd])


def ffn_phase(nc, G, dr, l, f, cfg, skip_ctx=False):
    sub = 0 if f == 0 else 2
    P = Phase(nc, f"ffn{l}{f}")
    N = 512
    XT = dr["XT"]
    w1 = P.sb("w1", [128, 8, 2 * DFF], BF16)
    w2 = P.sb("w2", [128, NJ, D], BF16)
    xt = P.sb("xt", [128, 8, N], F32)
    sg = P.sb("sg", [128, N], F32)
    h = [P.sb(f"h{i}", [128, 8, N], BF16) for i in range(2)]
    u = P.sb("u", [128, NJ, N], BF16)
    y = P.sb("y", [128, 8, N], F32)
    rstd = P.sb("rstd", [128, N], F32)
    ps_ss = P.ps("ss")
    ps_g = [P.ps(f"g{i}") for i in range(2)]
    ps_u = [P.ps(f"u{i}") for i in range(2)]
    ps_y = [P.ps(f"y{i}") for i in range(2)]
    JG = [(0, 2), (2, 8), (8, 15), (15, 22)]
    b_w1g = [P.buf() for _ in range(4)]
    b_w2 = P.buf()
    b_sg, b_u, b_rstd, b_ss = (P.buf() for _ in range(4))
    b_xt = [P.buf() for _ in range(8)]
    b_y = [P.buf() for _ in range(8)]
    b_h = [[P.buf() for _ in range(8)] for _ in range(2)]
    b_pg = [P.buf(), P.buf()]
    b_pu = [P.buf(), P.buf()]
    b_py = [P.buf(), P.buf()]
    s_w1 = [P.dsem_sw(f"w1{i}") for i in range(4)]
    s_w2 = P.dsem_sw("w2")
    s_ld = P.dsem("ld")
    s_st = P.dsem("st")
    ones = G["ones"]
    A, B, Gt = G["A"], G["B"], G["Gt"]
    P.track(G["b"])
    jgrp = {}
    for gi, (j0, j1) in enumerate(JG):
        for jj in range(j0, j1):
            jgrp[jj] = gi

    w1src = dr["w_ffn_in"][l, f].rearrange("(kc p) n -> p kc n", p=128)
    w2src = dr["w_ffn_out"][l, f].rearrange("(j p) n -> p j n", p=128)
    for gi, (j0, j1) in enumerate(JG):
        for off in (0, DFF):
            P.dma("pool", w1[:, :, off + j0 * 128:off + j1 * 128], w1src[:, :, off + j0 * 128:off + j1 * 128], s_w1[gi],
                  writes=[b_w1g[gi]])
    for hh in range(2):
        P.dma("pool", w2[:, hh * 11:(hh + 1) * 11, :], w2src[:, hh * 11:(hh + 1) * 11, :], s_w2, writes=[b_w2])

    tiles = token_tiles(N)
    if skip_ctx:
        tiles = [t for t in tiles if t[2] == 0]
    if cfg.ffn_tiles is not None:
        tiles = tiles[:cfg.ffn_tiles]
    nt = len(tiles)

    XS = dr["XTin"] if (l == 0 and f == 0 and cfg.phases is None) else XT

    def xsrc(ti):
        t0, n, j = tiles[ti]
        return XS[:, t0:t0 + n].rearrange("(kc p) t -> p kc t", p=128)

    def xdst(ti):
        t0, n, j = tiles[ti]
        return XT[:, t0:t0 + n].rearrange("(kc p) t -> p kc t", p=128)

    def H_load(ti):
        t0, n, j = tiles[ti]
        P.dma("sp", xt[:, :, :n], xsrc(ti), s_ld, writes=b_xt)

    def H_sq(ti):
        t0, n, j = tiles[ti]
        hb = ti % 2
        P.op("act", lambda e: e.activation(h[hb][:, :, :n], xt[:, :, :n], AF.Square), reads=b_xt, writes=b_h[hb])

    def H_pre(ti):
        t0, n, j = tiles[ti]
        hb = ti % 2
        for kc in range(8):
            P.mm(ps_ss[:, :n], ones[:], h[hb][:, kc, :n], start=(kc == 0), stop=(kc == 7), reads=[b_h[hb][kc]], writes=[b_ss])
        rms_rstd(P, ps_ss, rstd, n, b_ss, b_rstd, G["epsb"])

    def H_c(ti, kc):
        t0, n, j = tiles[ti]
        hb = ti % 2
        P.op("dve", lambda e: e.tensor_tensor(xt[:, kc, :n], xt[:, kc, :n], rstd[:, :n], ALU.mult),
             reads=[b_xt[kc], b_rstd], writes=[b_xt[kc]])
        P.op("act", lambda e: e.activation(
            h[hb][:, kc, :n], xt[:, kc, :n], AF.Identity, bias=B[:, l, sub, j, kc:kc + 1], scale=A[:, l, sub, j, kc:kc + 1]),
            reads=[b_xt[kc], G["b"]], writes=[b_h[hb][kc]])

    def M_gu(ti, jj):
        t0, n, j = tiles[ti]
        hb = ti % 2
        pb = jj % 2
        bw = b_w1g[jgrp[jj]]
        for kc in range(8):
            P.mm(ps_g[pb][:, :n], w1[:, kc, jj * 128:(jj + 1) * 128], h[hb][:, kc, :n], start=(kc == 0), stop=(kc == 7),
                 reads=[bw, b_h[hb][kc]], writes=[b_pg[pb]])
        for kc in range(8):
            P.mm(ps_u[pb][:, :n], w1[:, kc, DFF + jj * 128:DFF + (jj + 1) * 128], h[hb][:, kc, :n], start=(kc == 0),
                 stop=(kc == 7), reads=[bw, b_h[hb][kc]], writes=[b_pu[pb]])
        P.op("act", lambda e: e.activation(sg[:, :n], ps_g[pb][:, :n], AF.Silu), reads=[b_pg[pb]], writes=[b_sg])
        P.op("dve", lambda e: e.tensor_tensor(u[:, jj, :n], sg[:, :n], ps_u[pb][:, :n], ALU.mult),
             reads=[b_sg, b_pu[pb]], writes=[b_u])

    def M_y(ti, c):
        t0, n, j = tiles[ti]
        hb = ti % 2
        pb = c % 2
        for jj in range(NJ):
            P.mm(ps_y[pb][:, :n], w2[:, jj, c * 128:(c + 1) * 128], u[:, jj, :n], start=(jj == 0), stop=(jj == NJ - 1),
                 reads=[b_w2, b_u], writes=[b_py[pb]])
        P.op("act", lambda e: e.activation(h[hb][:, c, :n], ps_y[pb][:, :n], AF.Square), reads=[b_py[pb]], writes=[b_h[hb][c]])
        P.op("act", lambda e: e.activation(y[:, c, :n], ps_y[pb][:, :n], AF.Identity, scale=Gt[:, l, sub, j, c:c + 1]),
             reads=[b_py[pb], G["b"]], writes=[b_y[c]])

    def T_reload(ti):
        t0, n, j = tiles[ti]
        P.dma("sp", xt[:, :, :n], xsrc(ti), s_ld, writes=b_xt)

    def T_pre(ti):
        t0, n, j = tiles[ti]
        hb = ti % 2
        for c in range(8):
            P.mm(ps_ss[:, :n], ones[:], h[hb][:, c, :n], start=(c == 0), stop=(c == 7), reads=[b_h[hb][c]], writes=[b_ss])
        rms_rstd(P, ps_ss, rstd, n, b_ss, b_rstd, G["epsb"])

    def T_c(ti, c):
        t0, n, j = tiles[ti]
        P.op("dve", lambda e: e.tensor_tensor(y[:, c, :n], y[:, c, :n], rstd[:, :n], ALU.mult),
             reads=[b_y[c], b_rstd], writes=[b_y[c]])
        P.op("pool", lambda e: e.tensor_tensor(xt[:, c, :n], xt[:, c, :n], y[:, c, :n], ALU.add),
             reads=[b_y[c], b_xt[c]], writes=[b_xt[c]])

    def T_store(ti):
        t0, n, j = tiles[ti]
        P.dma("sp", xdst(ti), xt[:, :, :n], s_st, reads=b_xt)

    H_load(0)
    H_sq(0)
    H_pre(0)
    for kc in range(8):
        H_c(0, kc)
    if nt == 1:
        T_reload(0)
    for ti in range(nt):
        nxt = ti + 1 < nt
        for jj in range(NJ):
            M_gu(ti, jj)
            if ti >= 1:
                if jj == 1:
                    T_pre(ti - 1)
                if 2 <= jj <= 9:
                    T_c(ti - 1, jj - 2)
                if jj == 10:
                    T_store(ti - 1)
                if jj == 11 and not nxt:
                    T_reload(ti)
            if nxt:
                if jj == 10:
                    H_load(ti + 1)
                if jj == 16:
                    H_sq(ti + 1)
                if jj == 17:
                    H_pre(ti + 1)
                if jj >= 18:
                    H_c(ti + 1, jj - 18)
        for c in range(8):
            M_y(ti, c)
            if nxt and c < 4:
                H_c(ti + 1, 4 + c)
            if nxt and c == 3:
                T_reload(ti)
    T_pre(nt - 1)
    for c in range(8):
        T_c(nt - 1, c)
    T_store(nt - 1)
    P.finish()


def norm_mod_h(P, G, X, xsq, ps_ss, rstd, tmp, h, n, l, sub, j, b_x, b_xsq, b_ss, b_rstd, b_tmp, b_h):
    ones, A, B = G["ones"], G["A"], G["B"]
    P.op("act", lambda e: e.activation(xsq[:, :, :n], X[:, :, :n], AF.Square), reads=[b_x], writes=[b_xsq])
    for kc in range(8):
        P.mm(ps_ss[:, :n], ones[:], xsq[:, kc, :n], start=(kc == 0), stop=(kc == 7), reads=[b_xsq], writes=[b_ss])
    rms_rstd(P, ps_ss, rstd, n, b_ss, b_rstd, G["epsb"])
    for kc in range(8):
        k2 = kc % 2
        P.op("dve", lambda e, kc=kc, k2=k2: e.tensor_tensor(tmp[k2][:, :n], X[:, kc, :n], rstd[:, :n], ALU.mult),
             reads=[b_x, b_rstd], writes=[b_tmp[k2]])
        P.op("act", lambda e, kc=kc, k2=k2: e.activation(
            h[:, kc, :n], tmp[k2][:, :n], AF.Identity, bias=B[:, l, sub, j, kc:kc + 1], scale=A[:, l, sub, j, kc:kc + 1]),
            reads=[b_tmp[k2], G["b"]], writes=[b_h])


def proj_phase(nc, G, dr, l, cfg):
    P = Phase(nc, f"proj{l}")
    N = 512
    XT, QK, VT = dr["XT"], dr["QK"], dr["VT"]
    wf = P.sb("wf", [128, 8, 1536], BF16)
    wp = P.sb("wp", [128, 8, 512], BF16)
    wv = P.sb("wv", [128, 8, 512], BF16)
    xt = [P.sb(f"xt{i}", [128, 8, N], F32) for i in range(2)]
    xsq2 = [P.sb(f"xsq{i}", [128, 8, N], BF16) for i in range(2)]
    tmp = [P.sb(f"tmp{i}", [128, N], F32) for i in range(2)]
    h2 = [P.sb(f"h{i}", [128, 8, N], BF16) for i in range(2)]
    rstd = P.sb("rstd", [128, N], F32)
    cs = [P.sb(f"cs{i}", [128, 2, N], F32) for i in range(2)]
    stg = [P.sb(f"stg{i}", [128, 12, N], BF16) for i in range(2)]
    vst = [P.sb(f"vst{i}", [128, 4, 512], BF16) for i in range(2)]
    t1 = [P.sb(f"t1{i}", [128, N], F32) for i in range(2)]
    t2 = [P.sb(f"t2{i}", [128, N], F32) for i in range(2)]
    ps_ss = P.ps("ss")
    ps_f = [P.ps(f"f{i}") for i in range(3)]
    ps_p = [P.ps(f"p{i}") for i in range(2)]
    ps_v = [P.ps(f"v{i}") for i in range(2)]
    b_w = P.buf()
    b_xt = [P.buf(), P.buf()]
    b_cs = [P.buf(), P.buf()]
    b_stg = [P.buf(), P.buf()]
    b_vst = [P.buf(), P.buf()]
    b_rstd, b_ss = P.buf(), P.buf()
    b_xsq2 = [P.buf(), P.buf()]
    b_h2 = [[P.buf() for _ in range(8)] for _ in range(2)]
    b_tmp = [P.buf(), P.buf()]
    b_t1 = [P.buf(), P.buf()]
    b_t2 = [P.buf(), P.buf()]
    b_pf = [P.buf() for _ in range(3)]
    b_pp = [P.buf() for _ in range(2)]
    b_pv = [P.buf() for _ in range(2)]
    s_w = P.dsem_sw("w")
    s_ld = [P.dsem("ld0"), P.dsem("ld1")]
    s_cs = [P.dsem("cs0"), P.dsem("cs1")]
    s_st = [P.dsem("st0"), P.dsem("st1")]
    s_sv = [P.dsem("sv0"), P.dsem("sv1")]
    P.track(G["b"])
    wfsrc = dr["w_in_fm"][l].rearrange("(kc p) n -> p kc n", p=128)
    b_wg = [P.buf() for _ in range(4)]
    s_wg = [P.dsem_sw(f"wg{i}") for i in range(4)]
    for gi in range(4):
        P.dma("pool", wf[:, :, gi * 384:(gi + 1) * 384], wfsrc[:, :, gi * 384:(gi + 1) * 384], s_wg[gi], writes=[b_wg[gi]])
    P.dma("pool", wp[:], dr["w_in_pp"][l].rearrange("(kc p) n -> p kc n", p=128), s_w, writes=[b_w])
    P.dma("pool", wv[:], dr["w_in_v"][l].rearrange("(kc p) n -> p kc n", p=128), s_w, writes=[b_w])
    tiles = token_tiles(N)
    if cfg.proj_tiles is not None:
        tiles = tiles[:cfg.proj_tiles]
    def loads(ti):
        t0, n, j = tiles[ti]
        sl = ti % 2
        P.dma("sp", xt[sl][:, :, :n], XT[:, t0:t0 + n].rearrange("(kc p) t -> p kc t", p=128), s_ld[sl], writes=[b_xt[sl]])

    def load_cs(ti):
        t0, n, j = tiles[ti]
        sl = ti % 2
        if j == 0:
            P.dma("sp", cs[sl][:, :, :n], dr["rope_cs"][:, :, t0:t0 + n].rearrange("c p t -> p c t"), s_cs[sl], writes=[b_cs[sl]])

    ones, A, B = G["ones"], G["A"], G["B"]

    def H_sq(ti):
        t0, n, j = tiles[ti]
        sl = ti % 2
        P.op("act", lambda e: e.activation(xsq2[sl][:, :, :n], xt[sl][:, :, :n], AF.Square), reads=[b_xt[sl]], writes=[b_xsq2[sl]])

    def H_pre(ti):
        t0, n, j = tiles[ti]
        sl = ti % 2
        for kc in range(8):
            P.mm(ps_ss[:, :n], ones[:], xsq2[sl][:, kc, :n], start=(kc == 0), stop=(kc == 7), reads=[b_xsq2[sl]], writes=[b_ss])
        rms_rstd(P, ps_ss, rstd, n, b_ss, b_rstd, G["epsb"])

    def H_c(ti, kc):
        t0, n, j = tiles[ti]
        sl = ti % 2
        k2 = kc % 2
        P.op("dve", lambda e: e.tensor_tensor(tmp[k2][:, :n], xt[sl][:, kc, :n], rstd[:, :n], ALU.mult),
             reads=[b_xt[sl], b_rstd], writes=[b_tmp[k2]])
        P.op("act", lambda e: e.activation(
            h2[sl][:, kc, :n], tmp[k2][:, :n], AF.Identity, bias=B[:, l, 1, j, kc:kc + 1], scale=A[:, l, 1, j, kc:kc + 1]),
            reads=[b_tmp[k2], G["b"]], writes=[b_h2[sl][kc]])

    loads(0)
    load_cs(0)
    H_sq(0)
    H_pre(0)
    for kc in range(8):
        H_c(0, kc)
    if len(tiles) > 1:
        loads(1)
    for ti, (t0, n, j) in enumerate(tiles):
        sl = ti % 2
        h = h2[sl]
        b_hc = b_h2[sl]
        nxt = ti + 1 < len(tiles)
        if nxt:
            load_cs(ti + 1)
        for ch in range(12):
            r = ch % 3
            for kc in range(8):
                P.mm(ps_f[r][:, :n], wf[:, kc, ch * 128:(ch + 1) * 128], h[:, kc, :n], start=(kc == 0), stop=(kc == 7),
                     reads=[b_wg[ch // 3], b_hc[kc]], writes=[b_pf[r]])
            if 6 <= ch <= 9 and j == 0:
                r2 = ch % 2
                for kc in range(8):
                    P.mm(ps_p[r2][:, :n], wp[:, kc, (ch - 6) * 128:(ch - 5) * 128], h[:, kc, :n], start=(kc == 0), stop=(kc == 7),
                         reads=[b_w, b_hc[kc]], writes=[b_pp[r2]])
                P.op("dve", lambda e, r=r, r2=r2, sl=sl, n=n: e.tensor_tensor(t1[r2][:, :n], ps_f[r][:, :n], cs[sl][:, 0, :n], ALU.mult),
                     reads=[b_pf[r], b_cs[sl]], writes=[b_t1[r2]])
                P.op("dve", lambda e, r2=r2, sl=sl, n=n: e.tensor_tensor(t2[r2][:, :n], ps_p[r2][:, :n], cs[sl][:, 1, :n], ALU.mult),
                     reads=[b_pp[r2], b_cs[sl]], writes=[b_t2[r2]])
                P.op("pool", lambda e, r2=r2, sl=sl, ch=ch, n=n: e.tensor_tensor(stg[sl][:, ch, :n], t1[r2][:, :n], t2[r2][:, :n], ALU.add),
                     reads=[b_t1[r2], b_t2[r2]], writes=[b_stg[sl]])
            else:
                P.op("act", lambda e, r=r, sl=sl, ch=ch, n=n: e.activation(stg[sl][:, ch, :n], ps_f[r][:, :n], AF.Identity),
                     reads=[b_pf[r]], writes=[b_stg[sl]])
            if nxt:
                if ch == 0:
                    H_sq(ti + 1)
                elif ch == 1:
                    H_pre(ti + 1)
                elif ch <= 9:
                    H_c(ti + 1, ch - 2)
            if ch == 10 and ti + 2 < len(tiles):
                loads(ti + 2)
        P.dma("sp", QK[:, :, t0:t0 + n].rearrange("c p t -> p c t"), stg[sl][:, :, :n], s_st[sl], reads=[b_stg[sl]])
        nst = n // 128
        for st in range(nst):
            r = st % 2
            for kc in range(8):
                P.mm(ps_v[r][:, :], h[:, kc, st * 128:(st + 1) * 128], wv[:, kc, :], start=(kc == 0), stop=(kc == 7),
                     reads=[b_w, b_hc[kc]], writes=[b_pv[r]])
            P.op("act", lambda e, r=r, sl=sl, st=st: e.activation(vst[sl][:, st, :], ps_v[r][:, :], AF.Identity),
                 reads=[b_pv[r]], writes=[b_vst[sl]])
        tc0 = t0 // 128
        P.dma("sp", VT[tc0:tc0 + nst].rearrange("c p f -> p c f"), vst[sl][:, :nst, :], s_sv[sl], reads=[b_vst[sl]])
    P.finish()


def na_chunks(t):
    if t >= 32:
        return [], 0
    if t == 0:
        return [0, 1, 2, 3], 3
    if t == 1:
        return [0, 1, 2, 3], 2
    if t == 30:
        return [28, 29, 30, 31], 1
    if t == 31:
        return [28, 29, 30, 31], 0
    return [t - 2, t - 1, t, t + 1, t + 2], 7


def attnA_phase(nc, G, dr, l, cfg):
    P = Phase(nc, f"attA{l}")
    QK, VT, OT = dr["QK"], dr["VT"], dr["OT"]
    nqt = 34 if l < DEPTH - 1 else 32
    if cfg.att_tiles is not None:
        nqt = cfg.att_tiles
    ones = G["ones"]
    QTz = [P.sb(f"qtz{i}", [128, 2, NTOK], BF16) for i in range(2)]
    KT = [P.sb(f"kt{i}", [128, NTOK], BF16) for i in range(2)]
    V = [P.sb(f"v{i}", [128, 34, 128], BF16) for i in range(2)]
    bias = [P.sb(f"bias{i}", [128, 12, 256], F32) for i in range(2)]
    ost = [P.sb(f"ost{i}", [128, NTOK], BF16) for i in range(2)]
    ebias = [P.sb(f"ebias{i}", [128, 12, 256], BF16) for i in range(2)]
    b_eb = [P.buf(), P.buf()]
    El = [P.sb(f"el{i}", [128, 1280], BF16) for i in range(3)]
    Ec = [P.sb(f"ec{i}", [128, 512], BF16) for i in range(3)]
    rd = P.sb("rd", [128, 256], F32)
    ps_sl = P.ps("sl", [128, 1536])
    ps_sc = P.ps("sc")
    ps_o = [P.ps("o0"), P.ps("o1")]
    ps_d = [P.ps("d0"), P.ps("d1")]
    b_in = [P.buf(), P.buf()]
    b_ost = [P.buf(), P.buf()]
    b_sl = [P.buf(), P.buf()]
    b_El = [P.buf() for _ in range(3)]
    b_Ec = [P.buf() for _ in range(3)]
    b_psl, b_psc, b_rd = P.buf(), P.buf(), P.buf()
    b_po = [P.buf(), P.buf()]
    b_pd = [P.buf(), P.buf()]
    s_in = [P.dsem("in0"), P.dsem("in1")]
    s_out = [P.dsem("out0"), P.dsem("out1")]
    npairs = 3 if cfg.att_pairs is None else cfg.att_pairs
    for i in range(2):
        P.op("dve", lambda e, i=i: e.memset(QTz[i][:], 0.0), writes=[b_in[i]])

    def pair_loads(hp):
        s = hp % 2
        P.dma("sp", QTz[s][0:64, 0, :], QK[hp][0:64, :], s_in[s], writes=[b_in[s]])
        P.dma("sp", QTz[s][64:128, 1, :], QK[hp][64:128, :], s_in[s], writes=[b_in[s]])
        P.dma("sp", KT[s][:], QK[3 + hp], s_in[s], writes=[b_in[s]])
        P.dma("sp", V[s][:], VT[:, :, hp * 128:(hp + 1) * 128].rearrange("c p f -> p c f"), s_in[s], writes=[b_in[s]])
        P.dma("sp", bias[s][:], dr["na_bias"][l][:, hp].rearrange("p c e q -> p c (e q)"), s_in[s], writes=[b_in[s]])

    def pair_ebias(hp):
        s = hp % 2
        P.op("act", lambda e: e.activation(ebias[s][:], bias[s][:], AF.Exp), reads=[b_in[s]], writes=[b_eb[s]])

    pair_loads(0)
    pair_ebias(0)
    for hp in range(npairs):
        s = hp % 2
        if hp + 1 < npairs:
            pair_loads(hp + 1)

        def s_stage(t, s=s):
            r, r3 = t % 2, t % 3
            chunks, i0 = na_chunks(t)
            nl = len(chunks)
            q_ap = QTz[s][:, :, t * 128:(t + 1) * 128]
            for ii, kc in enumerate(chunks):
                P.mm(ps_sl[:, ii * 256:(ii + 1) * 256].rearrange("p (e q) -> p e q", e=2), KT[s][:, kc * 128:(kc + 1) * 128], q_ap,
                     start=True, stop=True, reads=[b_in[s]], writes=[b_psl])
            for ii, kc in enumerate((32, 33)):
                P.mm(ps_sc[:, ii * 256:(ii + 1) * 256].rearrange("p (e q) -> p e q", e=2), KT[s][:, kc * 128:(kc + 1) * 128], q_ap,
                     start=True, stop=True, reads=[b_in[s]], writes=[b_psc])
            if nl:
                P.op("act", lambda en: en.activation(El[r3][:, :nl * 256], ps_sl[:, :nl * 256], AF.Exp, scale=0.125),
                     reads=[b_psl], writes=[b_El[r3]])
                P.op("dve", lambda en: en.tensor_tensor(
                    El[r3][:, :nl * 256], El[r3][:, :nl * 256], ebias[s][:, i0:i0 + nl, :].rearrange("p a b -> p (a b)"), ALU.mult),
                    reads=[b_El[r3], b_eb[s]], writes=[b_El[r3]])
            P.op("act", lambda en: en.activation(Ec[r3][:, :], ps_sc[:, :512], AF.Exp, scale=0.125),
                 reads=[b_psc], writes=[b_Ec[r3]])

        def pv_stage(t, s=s):
            r3 = t % 3
            ob = t % 2
            chunks, i0 = na_chunks(t)
            srcs = [(El[r3][:, ii * 256:(ii + 1) * 256], kc, b_El[r3]) for ii, kc in enumerate(chunks)]
            srcs += [(Ec[r3][:, ii * 256:(ii + 1) * 256], kc, b_Ec[r3]) for ii, kc in enumerate((32, 33))]
            for idx, (ap, kc, bb) in enumerate(srcs):
                P.mm(ps_o[ob][:, 0:256], V[s][:, kc, :], ap, start=(idx == 0), stop=(idx == len(srcs) - 1),
                     reads=[bb, b_in[s]], writes=[b_po[ob]])
            for idx, (ap, kc, bb) in enumerate(srcs):
                P.mm(ps_d[ob][:, 0:256], ones[:, :], ap, start=(idx == 0), stop=(idx == len(srcs) - 1),
                     reads=[bb], writes=[b_pd[ob]])
            P.op("act", lambda en: en.activation(rd[:, :], ps_d[ob][:, 0:256], AF.Ln), reads=[b_pd[ob]], writes=[b_rd])
            P.op("act", lambda en: en.activation(rd[:, :], rd[:, :], AF.Exp, scale=-1.0), reads=[b_rd], writes=[b_rd])
            P.op("dve", lambda en: en.tensor_tensor(ost[s][0:64, t * 128:(t + 1) * 128], ps_o[ob][0:64, 0:128], rd[0:64, 0:128], ALU.mult),
                 reads=[b_po[ob], b_rd], writes=[b_ost[s]])
            P.op("dve", lambda en: en.tensor_tensor(ost[s][64:128, t * 128:(t + 1) * 128], ps_o[ob][64:128, 128:256], rd[64:128, 128:256], ALU.mult),
                 reads=[b_po[ob], b_rd], writes=[b_ost[s]])

        for t in range(nqt + 1):
            if t < nqt:
                s_stage(t)
            if t >= 1:
                pv_stage(t - 1)
            if t == 12 and hp + 1 < npairs:
                pair_ebias(hp + 1)
        P.dma("sp", OT[hp][:, :nqt * 128], ost[s][:, :nqt * 128], s_out[s], reads=[b_ost[s]])
    P.finish()


def attnB_phase(nc, G, dr, l, cfg):
    P = Phase(nc, f"attB{l}")
    QK, VT, OT = dr["QK"], dr["VT"], dr["OT"]
    nqb = 34 if l < DEPTH - 1 else 32
    if cfg.att_tiles is not None:
        nqb = cfg.att_tiles
    QTz = P.sb("qtz", [128, 2, 3, NTOK], BF16)
    KT = P.sb("kt", [128, NTOK], BF16)
    Va = [P.sb(f"va{i}", [128, 34, 128], BF16) for i in range(2)]
    msk = P.sb("msk", [128, 2, 384], BF16)
    sink = P.sb("sink", [128, 6], F32)
    esink = P.sb("esink", [128, 6], F32)
    swp = P.sb("swp", [128, 128], F32)
    ost = P.sb("ost", [128, 3, NTOK], BF16)
    sbs = [P.sb(f"sbs{i}", [128, 384], F32) for i in range(2)]
    E = [P.sb(f"e{i}", [128, 384], BF16) for i in range(6)]
    Dn2 = [P.sb(f"dn{i}", [128, 384], F32) for i in range(2)]
    rd2 = [P.sb(f"rd{i}", [128, 384], F32) for i in range(2)]
    esf = P.sb("esf", [128, 384], F32)
    ps_s = [P.ps(f"s{i}") for i in range(3)]
    ps_A = [P.ps("A0"), P.ps("A1")]
    ps_B = [P.ps("B0"), P.ps("B1")]
    ps_w = P.ps("w")
    b_in, b_sink, b_esink, b_ost, b_pw, b_esf = (P.buf() for _ in range(6))
    b_rd2 = [P.buf(), P.buf()]
    b_dn2 = [P.buf(), P.buf()]
    b_va = [P.buf(), P.buf()]
    b_sbs = [P.buf(), P.buf()]
    b_E = [P.buf() for _ in range(6)]
    b_ps = [P.buf() for _ in range(3)]
    b_pA = [P.buf(), P.buf()]
    b_pB = [P.buf(), P.buf()]
    s_in = P.dsem("in")
    s_v = P.dsem("v")
    s_out = P.dsem("out")
    P.op("dve", lambda e: e.memset(QTz[:], 0.0), writes=[b_in])
    for i in range(3):
        P.dma("sp", QTz[0:64, 0, i, :], QK[6 + i][0:64, :], s_in, writes=[b_in])
        P.dma("sp", QTz[64:128, 1, i, :], QK[6 + i][64:128, :], s_in, writes=[b_in])
    P.dma("sp", KT[:], QK[9], s_in, writes=[b_in])
    s_v2 = [s_v, P.dsem("v1")]
    for i in range(2):
        P.dma("sp", Va[i][:], VT[:, :, 384:512].rearrange("c p f -> p c f"), s_v2[i], writes=[b_va[i]])
    P.dma("sp", msk[:], dr["swa_mask01"][:], s_in, writes=[b_in])
    P.dma("sp", swp[:], dr["swapm"][:], s_in, writes=[b_in])
    P.dma("sp", sink[:], dr["sinkb"][l], P.dsem("sink"), writes=[b_sink])
    P.op("act", lambda e: e.activation(esink[:], sink[:], AF.Exp), reads=[b_sink], writes=[b_esink])
    P.op("dve", lambda e: e.memset(esf[:], 0.0), writes=[b_esf])
    for a in range(3):
        P.op("dve", lambda e, a=a: e.tensor_scalar(esf[0:64, a * 128:(a + 1) * 128], esf[0:64, a * 128:(a + 1) * 128],
                                                     esink[0:64, 3 + a:4 + a], None, ALU.add), reads=[b_esink, b_esf], writes=[b_esf])
        P.op("dve", lambda e, a=a: e.tensor_scalar(esf[64:128, a * 128:(a + 1) * 128], esf[64:128, a * 128:(a + 1) * 128],
                                                     esink[64:128, a:a + 1], None, ALU.add), reads=[b_esink, b_esf], writes=[b_esf])
    P.op("pool", lambda e: e.memset(Va[0][:, :, 64:128], 1.0), reads=[b_va[0]], writes=[b_va[0]])
    P.op("pool", lambda e: e.memset(Va[1][:, :, 0:64], 1.0), reads=[b_va[1]], writes=[b_va[1]])
    units = []
    for n in range(nqb):
        for kv in range(2):
            if n < 32:
                cl = []
                if n - 1 >= 0:
                    cl.append((n - 1, 0))
                cl.append((n, None))
                if n + 1 < 32:
                    cl.append((n + 1, 1))
                cl += [(32, None), (33, None)]
            else:
                cl = [(32, None), (33, None)]
            for ci, (kc, mi) in enumerate(cl):
                units.append((n, kv, kc, mi, ci == 0, ci == len(cl) - 1))

    def s_stage(i):
        n, kv, kc, mi, first, last = units[i]
        r, r2, r4 = i % 3, i % 2, i % 6
        P.mm(ps_s[r][:, 0:384].rearrange("p (a q) -> p a q", a=3), KT[:, kc * 128:(kc + 1) * 128],
             QTz[:, kv, :, n * 128:(n + 1) * 128], start=True, stop=True, reads=[b_in], writes=[b_ps[r]])
        P.op("act", lambda en: en.activation(E[r4][:, :], ps_s[r][:, 0:384], AF.Exp, scale=0.125),
             reads=[b_ps[r]], writes=[b_E[r4]])
        if mi is not None:
            P.op("dve", lambda en: en.tensor_tensor(E[r4][:, :], E[r4][:, :], msk[:, mi, :], ALU.mult),
                 reads=[b_E[r4], b_in], writes=[b_E[r4]])

    def pv_stage(i):
        n, kv, kc, mi, first, last = units[i]
        r4 = i % 6
        ob = n % 2
        acc, bacc = (ps_A[ob], b_pA[ob]) if kv == 0 else (ps_B[ob], b_pB[ob])
        P.mm(acc[:, 0:384], Va[kv][:, kc, :], E[r4][:, :], start=first, stop=last, reads=[b_E[r4], b_va[kv]], writes=[bacc])
        return last and kv == 1

    def F1(n):
        ob = n % 2
        A, B_ = ps_A[ob], ps_B[ob]
        P.op("dve", lambda en: en.tensor_tensor(Dn2[ob][0:64, :], B_[0:64, 0:384], esf[0:64, :], ALU.add),
             reads=[b_pB[ob], b_esf], writes=[b_dn2[ob]])
        P.op("dve", lambda en: en.tensor_tensor(Dn2[ob][64:128, :], A[64:128, 0:384], esf[64:128, :], ALU.add),
             reads=[b_pA[ob], b_esf], writes=[b_dn2[ob]])

    def F2(n):
        ob = n % 2
        P.mm(ps_w[:, 0:384], swp[:, :], Dn2[ob][:, :], start=True, stop=True, reads=[b_dn2[ob], b_in], writes=[b_pw])
        P.op("act", lambda en: en.activation(rd2[ob][:, :], ps_w[:, 0:384], AF.Ln), reads=[b_pw], writes=[b_rd2[ob]])
        P.op("act", lambda en: en.activation(rd2[ob][:, :], rd2[ob][:, :], AF.Exp, scale=-1.0), reads=[b_rd2[ob]], writes=[b_rd2[ob]])

    def F3(n):
        ob = n % 2
        A, B_ = ps_A[ob], ps_B[ob]
        P.op("dve", lambda en: en.tensor_tensor(
            ost[0:64, :, n * 128:(n + 1) * 128], A[0:64, 0:384].rearrange("p (a q) -> p a q", a=3),
            rd2[ob][0:64, :].rearrange("p (a q) -> p a q", a=3), ALU.mult), reads=[b_pA[ob], b_rd2[ob]], writes=[b_ost])
        P.op("dve", lambda en: en.tensor_tensor(
            ost[64:128, :, n * 128:(n + 1) * 128], B_[64:128, 0:384].rearrange("p (a q) -> p a q", a=3),
            rd2[ob][64:128, :].rearrange("p (a q) -> p a q", a=3), ALU.mult), reads=[b_pB[ob], b_rd2[ob]], writes=[b_ost])

    LAG = 4
    pend = {}
    total = len(units) + LAG
    i = 0
    while i < total or pend:
        if i < len(units):
            s_stage(i)
        if LAG <= i < total:
            if pv_stage(i - LAG):
                nn = units[i - LAG][0]
                F1(nn)
                pend.setdefault(i + 2, []).append((F2, nn))
                pend.setdefault(i + 3, []).append((F3, nn))
        for fn, nn in pend.pop(i, []):
            fn(nn)
        i += 1
    for a in range(3):
        P.dma("sp", OT[3 + a][:, :nqb * 128], ost[:, a, :nqb * 128], s_out, reads=[b_ost])
    P.finish()


def fnet_phase(nc, G, dr, l, cfg):
    P = Phase(nc, f"fnet{l}")
    QK, OT = dr["QK"], dr["OT"]
    do_ctx = l < DEPTH - 1
    ntc = 34 if do_ctx else 32
    fuT = P.sb("fuT", [128, 2, NTOK], BF16)
    bd = P.sb("bd", [128, 256], BF16)
    C0 = P.sb("c0", [128, 32, 512], BF16)
    S0 = P.sb("s0", [128, 32, 512], BF16)
    cc = P.sb("cc", [128, 2, 2, 256], BF16)
    casa = P.sb("casa", [128, 2, 8], F32)
    ucs = P.sb("ucs", [128, 34, 2, 2, 128], BF16)
    PQ = [P.sb(f"pq{i}", [128, 32, 2, 128], BF16) for i in range(2)]
    tA = P.sb("tA", [128, 32, 128], BF16)
    tB = P.sb("tB", [128, 32, 128], BF16)
    ost = P.sb("ost", [128, 2, NTOK], BF16)
    ps_u = [P.ps(f"u{i}") for i in range(2)]
    ps_f = [P.ps(f"f{i}") for i in range(2)]
    b_in, b_dft, b_ucs, b_tA, b_tB, b_ost = (P.buf() for _ in range(6))
    b_PQ = [P.buf(), P.buf()]
    b_pu = [P.buf(), P.buf()]
    b_pf = [P.buf(), P.buf()]
    s_in = P.dsem("in")
    s_dft = P.dsem("dft")
    s_out = P.dsem("out")
    for fc in range(2):
        P.dma("sp", fuT[:, fc, :], QK[10 + fc], s_in, writes=[b_in])
    P.dma("sp", bd[:], dr["bd64"][:], s_in, writes=[b_in])
    P.dma("sp", casa[:], dr["casa"][:], s_in, writes=[b_in])
    P.dma("sp", cc[:], dr["dft_ctx"][:], s_in, writes=[b_in])
    P.dma("sp", C0[:], dr["dft0"][0].rearrange("(mc p) n -> p mc n", p=128), s_dft, writes=[b_dft])
    P.dma("sp", S0[:], dr["dft0"][1].rearrange("(mc p) n -> p mc n", p=128), s_dft, writes=[b_dft])
    k = 0
    for tc in range(ntc):
        for fc in range(2):
            r = k % 2
            k += 1
            P.mm(ps_u[r][:, 0:256], fuT[:, fc, tc * 128:(tc + 1) * 128], bd[:, :], start=True, stop=True, reads=[b_in], writes=[b_pu[r]])
            P.op("act", lambda e, r=r, tc=tc, fc=fc: e.activation(
                ucs[:, tc, fc, :, :].rearrange("p a b -> p (a b)"), ps_u[r][:, 0:256], AF.Identity), reads=[b_pu[r]], writes=[b_ucs])
    nts = 8 if cfg.fnet_tiles is None else cfg.fnet_tiles
    iters = [(nt, fc) for nt in range(nts) for fc in range(2)]

    def rot(k):
        nt, fc = iters[k]
        r = k % 2
        uc = ucs[:, 0:32, fc, 0, :]
        us = ucs[:, 0:32, fc, 1, :]
        if nt == 0:
            return uc, us, b_ucs
        ca = casa[:, 0, nt:nt + 1]
        sa = casa[:, 1, nt:nt + 1]
        P.op("act", lambda e: e.activation(tA[:], us, AF.Identity, scale=sa), reads=[b_ucs, b_in], writes=[b_tA])
        P.op("dve", lambda e: e.scalar_tensor_tensor(PQ[r][:, :, 0, :], uc, ca, tA[:], ALU.mult, ALU.subtract),
             reads=[b_ucs, b_tA, b_in], writes=[b_PQ[r]])
        P.op("act", lambda e: e.activation(tB[:], uc, AF.Identity, scale=sa), reads=[b_ucs, b_in], writes=[b_tB])
        P.op("dve", lambda e: e.scalar_tensor_tensor(PQ[r][:, :, 1, :], us, ca, tB[:], ALU.mult, ALU.add),
             reads=[b_ucs, b_tB, b_in], writes=[b_PQ[r]])
        return PQ[r][:, :, 0, :], PQ[r][:, :, 1, :], b_PQ[r]

    def mmk(k, ops):
        nt, fc = iters[k]
        r = k % 2
        Pm, Qm, bb = ops
        for mc in range(32):
            P.mm(ps_f[r][:, :], Pm[:, mc, :], C0[:, mc, :], start=(mc == 0), stop=False, reads=[bb, b_dft], writes=[b_pf[r]])
            P.mm(ps_f[r][:, :], Qm[:, mc, :], S0[:, mc, :], start=False, stop=(mc == 31), reads=[bb, b_dft], writes=[b_pf[r]])
        P.op("act", lambda e: e.activation(ost[:, fc, nt * 512:(nt + 1) * 512], ps_f[r][:, :], AF.Identity, scale=1.0 / 512),
             reads=[b_pf[r]], writes=[b_ost])

    nxt_ops = rot(0)
    for k in range(len(iters)):
        cur = nxt_ops
        if k + 1 < len(iters):
            nxt_ops = rot(k + 1)
        mmk(k, cur)
    k = len(iters)
    if do_ctx:
        for fc in range(2):
            r = k % 2
            k += 1
            for mc in range(2):
                P.mm(ps_f[r][:, 0:256], ucs[:, 32 + mc, fc, 0, :], cc[:, mc, 0, :], start=(mc == 0), stop=False, reads=[b_ucs, b_in], writes=[b_pf[r]])
                P.mm(ps_f[r][:, 0:256], ucs[:, 32 + mc, fc, 1, :], cc[:, mc, 1, :], start=False, stop=(mc == 1), reads=[b_ucs, b_in], writes=[b_pf[r]])
            P.op("act", lambda e, r=r, fc=fc: e.activation(ost[:, fc, S:NTOK], ps_f[r][:, 0:256], AF.Identity, scale=1.0 / 128),
                 reads=[b_pf[r]], writes=[b_ost])
    for fc in range(2):
        P.dma("sp", OT[6 + fc][:, :ntc * 128], ost[:, fc, :ntc * 128], s_out, reads=[b_ost])
    P.finish()


def outproj_phase(nc, G, dr, l, cfg):
    P = Phase(nc, f"oproj{l}")
    N = 512
    sub = 1
    XT, OT = dr["XT"], dr["OT"]
    ones, Gt = G["ones"], G["Gt"]
    wo = P.sb("wo", [128, 8, D], BF16)
    NS = 3
    xt = [P.sb(f"xt{i}", [128, 8, N], F32) for i in range(NS)]
    ot = [P.sb(f"ot{i}", [128, 8, N], BF16) for i in range(NS)]
    xsq = P.sb("xsq", [128, 8, N], BF16)
    yb = [P.sb(f"y{i}", [128, 8, N], F32) for i in range(2)]
    rstd2 = P.sb("rstd2", [128, N], F32)
    tmp = [P.sb(f"tmp{i}", [128, N], F32) for i in range(2)]
    ps_ss = P.ps("ss")
    ps_y = [P.ps(f"y{i}") for i in range(2)]
    b_w = P.buf()
    b_xt = [[P.buf() for _ in range(8)] for _ in range(NS)]
    b_ot = [P.buf() for _ in range(NS)]
    b_xsq, b_rstd2, b_ss = (P.buf() for _ in range(3))
    b_yb = [[P.buf() for _ in range(8)] for _ in range(2)]
    b_tmp = [P.buf(), P.buf()]
    b_py = [P.buf(), P.buf()]
    s_w = P.dsem_sw("w")
    s_ld = [P.dsem(f"ld{i}") for i in range(NS)]
    s_ldo = [P.dsem(f"ldo{i}") for i in range(NS)]
    s_st = [P.dsem(f"st{i}") for i in range(NS)]
    P.track(G["b"])
    P.dma("pool", wo[:], dr["w_out_p"][l].rearrange("(kc p) n -> p kc n", p=128), s_w, writes=[b_w])
    tiles = token_tiles(N)
    if l == DEPTH - 1:
        tiles = [t for t in tiles if t[2] == 0]
    if cfg.proj_tiles is not None:
        tiles = tiles[:cfg.proj_tiles]
    def loads(ti):
        t0, n, j = tiles[ti]
        sl = ti % NS
        P.dma("sp", xt[sl][:, :, :n], XT[:, t0:t0 + n].rearrange("(kc p) t -> p kc t", p=128), s_ld[sl], writes=b_xt[sl])
        P.dma("sp", ot[sl][:, :, :n], OT[:, :, t0:t0 + n].rearrange("c p t -> p c t"), s_ldo[sl], writes=[b_ot[sl]])

    loads(0)
    if len(tiles) > 1:
        loads(1)
    for ti, (t0, n, j) in enumerate(tiles):
        sl = ti % NS
        X = xt[sl]
        y = yb[ti % 2]
        b_y = b_yb[ti % 2]
        if ti + 2 < len(tiles):
            loads(ti + 2)
        for c in range(8):
            pb = c % 2
            for kc in range(8):
                P.mm(ps_y[pb][:, :n], wo[:, kc, c * 128:(c + 1) * 128], ot[sl][:, kc, :n], start=(kc == 0), stop=(kc == 7),
                     reads=[b_w, b_ot[sl]], writes=[b_py[pb]])
            P.op("act", lambda e, pb=pb, c=c, n=n: e.activation(xsq[:, c, :n], ps_y[pb][:, :n], AF.Square), reads=[b_py[pb]], writes=[b_xsq])
            P.op("act", lambda e, pb=pb, c=c, j=j, n=n, y=y: e.activation(y[:, c, :n], ps_y[pb][:, :n], AF.Identity, scale=Gt[:, l, sub, j, c:c + 1]),
                 reads=[b_py[pb], G["b"]], writes=[b_y[c]])
        for c in range(8):
            P.mm(ps_ss[:, :n], ones[:], xsq[:, c, :n], start=(c == 0), stop=(c == 7), reads=[b_xsq], writes=[b_ss])
        rms_rstd(P, ps_ss, rstd2, n, b_ss, b_rstd2, G["epsb"])
        for c in range(8):
            k2 = c % 2
            P.op("dve", lambda e, c=c, n=n, y=y: e.tensor_tensor(y[:, c, :n], y[:, c, :n], rstd2[:, :n], ALU.mult),
                 reads=[b_y[c], b_rstd2], writes=[b_y[c]])
            P.op("pool", lambda e, X=X, c=c, n=n, y=y: e.tensor_tensor(X[:, c, :n], X[:, c, :n], y[:, c, :n], ALU.add),
                 reads=[b_y[c], b_xt[sl][c]], writes=[b_xt[sl][c]])
        P.dma("sp", XT[:, t0:t0 + n].rearrange("(kc p) t -> p kc t", p=128), X[:, :, :n], s_st[sl], reads=b_xt[sl])
    P.finish()


def build(cfg):
    nc = bass.Bass("TRN2", target_bir_lowering=False)
    _POOL[0] = SemPool(nc)
    allh = [x.h for x in _POOL[0].eng.values()] + [x.h for x in _POOL[0].dma] + [x.h for x in _POOL[0].swdma]
    for hsem in allh:
        nc.gpsimd.sem_clear(hsem)
    nc.all_engine_barrier()
    dr = {}

    def din(name, shape, dt=F32):
        dr[name] = nc.dram_tensor(name, list(shape), dt, kind="ExternalInput").ap()

    din("XTin", [D, NTOK])
    din("cin", [128, 8, 2])
    din("bmod", [DEPTH, 128, 72])
    din("gpp", [DEPTH, 128, 2, 3, 8])
    din("w_mod", [DEPTH, D, 9 * D])
    din("w_ffn_in", [DEPTH, 2, D, 2 * DFF])
    din("w_ffn_out", [DEPTH, 2, DFF, D])
    din("w_in_fm", [DEPTH, D, 1536])
    din("w_in_pp", [DEPTH, D, 512])
    din("w_in_v", [DEPTH, D, 512])
    din("w_out_p", [DEPTH, D, D])
    din("rope_cs", [2, 128, S])
    din("na_bias", [DEPTH, 128, 3, 12, 2, 128])
    din("swa_mask01", [128, 2, 384], BF16)
    din("sinkb", [DEPTH, 128, 6])
    din("swapm", [128, 128])
    din("bd64", [128, 256], BF16)
    din("casa", [128, 2, 8])
    din("dft_ctx", [128, 2, 2, 256], BF16)
    din("dft0", [2, S, 512], BF16)
    dr["XT"] = nc.dram_tensor("XTo", [D, NTOK], F32, kind="ExternalOutput").ap()
    skind = "ExternalOutput" if cfg.debug else "Internal"
    dr["QK"] = nc.dram_tensor("QK", [12, 128, NTOK], BF16, kind=skind).ap()
    dr["VT"] = nc.dram_tensor("VT", [34, 128, 512], BF16, kind=skind).ap()
    dr["OT"] = nc.dram_tensor("OT", [8, 128, NTOK], BF16, kind=skind).ap()

    gs = ExitStack()
    G = {}
    G["A"] = gs.enter_context(nc.sbuf_tensor("gA", [128, DEPTH, 3, 2, 8], F32))
    G["B"] = gs.enter_context(nc.sbuf_tensor("gB", [128, DEPTH, 3, 2, 8], F32))
    G["Gt"] = gs.enter_context(nc.sbuf_tensor("gG", [128, DEPTH, 3, 2, 8], F32))
    G["ones"] = gs.enter_context(nc.sbuf_tensor("ones", [128, 128], BF16))
    G["epsb"] = gs.enter_context(nc.sbuf_tensor("epsb", [128, 1], F32))
    G["b"] = Buf("modvec")

    P = Phase(nc, "init")
    s0 = P.dsem("cp")
    if cfg.phases is not None:
        for i in range(8):
            P.dma("sp", dr["XT"][i * 128:(i + 1) * 128, :], dr["XTin"][i * 128:(i + 1) * 128, :], s0)
    P.op("dve", lambda e: e.memset(G["ones"][:], 1.0))
    P.op("dve", lambda e: e.memset(G["epsb"][:], float(D * EPS)))
    P.finish()

    ph = cfg.phases
    mod_phase(nc, G, dr)
    for l in range(cfg.depth):
        last = l == DEPTH - 1
        if ph is None or "ffn1" in ph:
            ffn_phase(nc, G, dr, l, 0, cfg, skip_ctx=False)
        if ph is None or "proj" in ph:
            proj_phase(nc, G, dr, l, cfg)
        if ph is None or "attA" in ph:
            attnA_phase(nc, G, dr, l, cfg)
        if ph is None or "attB" in ph:
            attnB_phase(nc, G, dr, l, cfg)
        if ph is None or "fnet" in ph:
            fnet_phase(nc, G, dr, l, cfg)
        if ph is None or "oproj" in ph:
            outproj_phase(nc, G, dr, l, cfg)
        if ph is None or "ffn2" in ph:
            ffn_phase(nc, G, dr, l, 1, cfg, skip_ctx=last)
    for hsem in allh:
        nc.gpsimd.sem_clear(hsem)
    nc.all_engine_barrier()
    gs.close()
    return nc


def _consts():
    inv = (10000.0 ** (-np.arange(16, dtype=np.float64) / 16)).astype(np.float32).astype(np.float64)
    t = np.arange(S)
    rows = (t // 64).astype(np.float64)
    cols = (t % 64).astype(np.float64)
    cos = np.zeros((64, S))
    sin = np.zeros((64, S))
    for d in range(64):
        pos = rows if d < 32 else cols
        ang = pos * inv[d % 16]
        cos[d] = np.cos(ang)
        sin[d] = np.sin(ang) * (-1.0 if (d % 32) < 16 else 1.0)
    rope = np.stack([np.concatenate([cos, cos], 0), np.concatenate([sin, sin], 0)], 0).astype(np.float32)
    k = np.arange(128)[:, None]
    q = np.arange(128)[None, :]
    m0 = np.where(k >= q, 0.0, NEG)
    m1 = np.where(k <= q, 0.0, NEG)
    swa = (np.stack([np.tile(m0, (1, 3)), np.tile(m1, (1, 3))], 1) == 0.0).astype(np.float32).astype(ml_dtypes.bfloat16)
    kk = np.arange(64)
    c64 = np.cos(2 * np.pi * np.outer(kk, kk) / 64)
    s64 = np.sin(2 * np.pi * np.outer(kk, kk) / 64)
    bd = np.zeros((128, 256))
    for g in range(2):
        bd[g * 64:(g + 1) * 64, g * 64:(g + 1) * 64] = c64
        bd[g * 64:(g + 1) * 64, 128 + g * 64:128 + (g + 1) * 64] = s64
    p = np.arange(128)
    nt = np.arange(8)
    ang = 2 * np.pi * np.outer(p % 8, nt) / 8
    casa = np.stack([np.cos(ang), np.sin(ang)], 1).astype(np.float32)
    m = np.arange(256)
    angc = 2 * np.pi * np.outer(m, m) / 256
    cc = np.stack([np.cos(angc), -np.sin(angc)], 1)
    cc = cc.reshape(2, 128, 2, 256).transpose(1, 0, 2, 3)
    mm = np.arange(S)
    a0 = 2 * np.pi * np.outer(mm, np.arange(512)) / S
    dft0 = np.stack([np.cos(a0), -np.sin(a0)], 0)
    bf = ml_dtypes.bfloat16
    swapm = np.roll(np.eye(128, dtype=np.float32), 64, axis=1)
    return dict(swapm=swapm, rope_cs=rope, swa_mask01=swa, bd64=bd.astype(bf), casa=casa, dft_ctx=np.ascontiguousarray(cc).astype(bf),
                dft0=dft0.astype(bf))


def _na_bias_index():
    types = [(-6, 0), (-4, 0), (-2, 0), (0, 0), (2, 0), (4, 0), (6, 0), (-4, 1), (-2, 0), (0, 0), (2, 0), (4, 1)]
    a = (np.arange(128) // 64)[:, None]
    kc = (np.arange(128) % 64)[:, None]
    b = (np.arange(128) // 64)[None, :]
    qc = (np.arange(128) % 64)[None, :]
    ws = np.clip(qc - 8, 0, 48)
    colv = (kc >= ws) & (kc < ws + 16)
    ri = np.zeros((12, 128, 128), np.int64)
    ci = np.zeros((12, 128, 128), np.int64)
    va = np.zeros((12, 128, 128), bool)
    for i, (e, msk) in enumerate(types):
        drr = e + a - b
        rowv = np.ones((128, 128), bool)
        if msk and e == -4:
            rowv = a >= b
        if msk and e == 4:
            rowv = (a + 1) <= b
        v = colv & rowv & (np.abs(drr) <= 7)
        ri[i] = np.clip(drr + 7, 0, 14)
        ci[i] = np.clip(kc - qc + 15, 0, 30)
        va[i] = v
    return ri, ci, va


_CONST_CACHE = {}


def host_shared(inputs):
    if "c" not in _CONST_CACHE:
        _CONST_CACHE["c"] = _consts()
        _CONST_CACHE["nb"] = _na_bias_index()
    m = dict(_CONST_CACHE["c"])
    w_in = inputs["w_in"]
    hperm = np.array([i * 1 for i in range(64)])
    part = np.concatenate([np.arange(16, 32), np.arange(0, 16), np.arange(48, 64), np.arange(32, 48)])
    bq0 = 1152
    bk0 = 1536
    bq_cols = np.concatenate([np.concatenate([bq0 + i * 64 + hperm, bq0 + (3 + i) * 64 + hperm]) for i in range(3)])
    bq_pcols = np.concatenate([np.concatenate([bq0 + i * 64 + part, bq0 + (3 + i) * 64 + part]) for i in range(3)])
    bk_cols = bk0 + np.arange(128)
    bk_pcols = np.concatenate([bk0 + part, bk0 + 64 + part])
    fm_cols = np.concatenate([np.arange(0, 768), bq_cols, bk_cols, np.arange(1792, 2048)])
    pp_cols = np.concatenate([bq_pcols, bk_pcols])
    v_cols = np.concatenate([np.arange(768, 1152), np.arange(1664, 1792)])
    m["w_in_fm"] = w_in[:, :, fm_cols]
    m["w_in_pp"] = w_in[:, :, pp_cols]
    m["w_in_v"] = w_in[:, :, v_cols]
    orow = np.concatenate([np.arange(0, 384)] + [np.concatenate([384 + i * 64 + hperm, 384 + (3 + i) * 64 + hperm]) for i in range(3)]
                          + [np.arange(768, 1024)])
    m["w_out_p"] = inputs["w_out"][:, orow, :]
    ri, ci, va = _CONST_CACHE["nb"]
    rpb = inputs["na_rpb"]
    g = rpb[:, :, ri, ci]
    g = np.where(va[None, None], g, np.float32(NEG))
    g = g.reshape(DEPTH, 3, 2, 12, 128, 128)
    m["na_bias"] = g.transpose(0, 4, 1, 3, 2, 5)
    m["sinkb"] = np.broadcast_to(inputs["swa_sink"][:, None, :], (DEPTH, 128, 6))
    m["bmod"] = inputs["b_mod"].reshape(DEPTH, 72, 128).transpose(0, 2, 1)
    gpp = np.stack([inputs["g_pre"], inputs["g_post"]], axis=1)
    m["gpp"] = gpp.reshape(DEPTH, 2, 3, 8, 128).transpose(0, 4, 1, 2, 3)
    m["w_mod"] = inputs["w_mod"]
    m["w_ffn_in"] = inputs["w_ffn_in"]
    m["w_ffn_out"] = inputs["w_ffn_out"]
    out = {}
    for k2, v in m.items():
        if v.dtype == ml_dtypes.bfloat16:
            out[k2] = np.ascontiguousarray(v)
        else:
            out[k2] = np.ascontiguousarray(v, dtype=np.float32)
    return out


def host_inputs(inputs, b, shared):
    x, ctx, c, c_ctx = inputs["x"], inputs["ctx"], inputs["c"], inputs["c_ctx"]
    XT = np.ascontiguousarray(np.concatenate([x[b], ctx[b]], axis=0).T, dtype=np.float32)
    cin = np.ascontiguousarray(np.stack([c[b].reshape(8, 128).T, c_ctx.reshape(8, 128).T], axis=-1), dtype=np.float32)
    m = dict(shared)
    m["XTin"] = XT
    m["cin"] = cin
    return m


def run(inputs, cfg):
    inputs = {k: np.asarray(v) for k, v in inputs.items()}
    nc = build(cfg)
    shared = host_shared(inputs)
    in_maps = [host_inputs(inputs, b, shared) for b in range(cfg.ncores)]
    res = run_bass_kernel_spmd(nc, in_maps, core_ids=list(range(cfg.ncores)))
    return res


def kernel(**inputs):
    res = run(inputs, Cfg())
    out = np.stack([np.ascontiguousarray(r["XTo"][:, :S].T) for r in res.results], axis=0)
    return out.astype(np.float32)
```
